# Optimizing a Trainium2 kernel written in Bass

```python
import math
import jax, jax.numpy as jnp
from jax import lax
import numpy as np

D_MODEL = 1024
BATCH = 4
SEQ = 8192
DEPTH = 2

GRID_W = 64
CTX_LEN = 256
HEAD_DIM = 64
N_HEADS = D_MODEL // HEAD_DIM
HEADS_A = N_HEADS // 2
HEADS_B = N_HEADS - HEADS_A
GQA_GROUP = 4
KV_A = HEADS_A // GQA_GROUP
KV_B = HEADS_B // GQA_GROUP
WINDOW = 128
Q_BLOCK = 128
HEADS_C = D_MODEL // (2 * HEAD_DIM)
FFN_HIDDEN = ((8 * D_MODEL // 3 + 255) // 256) * 256
ROPE_THETA = 10000.0
EPS = 1e-6
NEG_INF = -1e30
N_EVEN = (DEPTH + 1) // 2
N_ODD = DEPTH // 2
EVEN_IN = (HEADS_A + 2 * KV_A + HEADS_B + 2 * KV_B) * HEAD_DIM
EVEN_OUT = (HEADS_A + HEADS_B) * HEAD_DIM
ODD_IN = 3 * 2 * HEADS_C * HEAD_DIM
ODD_OUT = HEADS_C * 2 * HEAD_DIM

kernel_name = "hybrid_dit_window_axial_diff_prefix"


def _rms(x, w):
    xf = x.astype(jnp.float32)
    y = xf * lax.rsqrt(jnp.mean(xf * xf, axis=-1, keepdims=True) + EPS)
    return (y * w.astype(jnp.float32)).astype(x.dtype)


def _axial_tables(rows):
    row = jnp.repeat(jnp.arange(rows, dtype=jnp.float32), GRID_W)
    col = jnp.tile(jnp.arange(GRID_W, dtype=jnp.float32), rows)
    n_freq = HEAD_DIM // 4
    inv = ROPE_THETA ** (-jnp.arange(n_freq, dtype=jnp.float32) / n_freq)
    ang = jnp.concatenate([row[:, None] * inv, col[:, None] * inv], axis=-1)
    return jnp.cos(ang), jnp.sin(ang)


def _rope(x, cos, sin):
    n = x.shape[1]
    bshape = (1, n) + (1,) * (x.ndim - 3) + (HEAD_DIM // 2,)
    c = cos.reshape(bshape).astype(x.dtype)
    s = sin.reshape(bshape).astype(x.dtype)
    xp = x.reshape(*x.shape[:-1], HEAD_DIM // 2, 2)
    x0, x1 = xp[..., 0], xp[..., 1]
    return jnp.stack([x0 * c - x1 * s, x0 * s + x1 * c], axis=-1).reshape(x.shape)


def _softmax(s, sink):
    if sink is None:
        return jax.nn.softmax(s, axis=-1)
    m = jnp.maximum(jnp.max(s, axis=-1, keepdims=True), sink)
    p = jnp.exp(s - m)
    return p / (jnp.sum(p, axis=-1, keepdims=True) + jnp.exp(sink - m))


def _to_blocks(q):
    b, n = q.shape[:2]
    return jnp.moveaxis(q.reshape(b, n // Q_BLOCK, Q_BLOCK, *q.shape[2:]), 1, 0)


def _from_blocks(o):
    o = jnp.moveaxis(o, 0, 1)
    return o.reshape(o.shape[0], o.shape[1] * o.shape[2], -1)


def _window_attend(q, k_lat, v_lat, k_ctx, v_ctx, sink):
    n = q.shape[1]
    n_ctx = k_ctx.shape[1]
    span = Q_BLOCK + 2 * WINDOW
    pad = ((0, 0), (WINDOW, WINDOW), (0, 0), (0, 0))
    k_pad = jnp.pad(k_lat, pad)
    v_pad = jnp.pad(v_lat, pad)
    scale = HEAD_DIM ** -0.5
    sink_b = sink.astype(jnp.float32)[None, :, :, None, None]

    def one_block(args):
        qb, i = args
        start = i * Q_BLOCK
        kw = lax.dynamic_slice_in_dim(k_pad, start, span, axis=1)
        vw = lax.dynamic_slice_in_dim(v_pad, start, span, axis=1)
        qpos = start + jnp.arange(Q_BLOCK)
        kpos = start - WINDOW + jnp.arange(span)
        ok = (jnp.abs(qpos[:, None] - kpos[None, :]) <= WINDOW) & (kpos[None, :] >= 0) & (kpos[None, :] < n)
        s_c = jnp.einsum('bqhgd,bkhd->bhgqk', qb, k_ctx).astype(jnp.float32) * scale
        s_w = jnp.einsum('bqhgd,bkhd->bhgqk', qb, kw).astype(jnp.float32) * scale
        s_w = jnp.where(ok, s_w, NEG_INF)
        p = _softmax(jnp.concatenate([s_c, s_w], axis=-1), sink_b).astype(v_lat.dtype)
        return (jnp.einsum('bhgqk,bkhd->bqhgd', p[..., :n_ctx], v_ctx)
                + jnp.einsum('bhgqk,bkhd->bqhgd', p[..., n_ctx:], vw))

    out = lax.map(one_block, (_to_blocks(q), jnp.arange(n // Q_BLOCK)))
    return _from_blocks(out)


def _dense_attend(q, k, v, sink):
    scale = HEAD_DIM ** -0.5
    sink_b = None if sink is None else sink.astype(jnp.float32)[None, :, :, None, None]

    def one_block(qb):
        s = jnp.einsum('bqhgd,bkhd->bhgqk', qb, k).astype(jnp.float32) * scale
        p = _softmax(s, sink_b).astype(v.dtype)
        return jnp.einsum('bhgqk,bkhd->bqhgd', p, v)

    return _from_blocks(lax.map(one_block, _to_blocks(q)))


def _diff_attend(q, k, v, lam):
    scale = HEAD_DIM ** -0.5

    def one_block(qb):
        s = jnp.einsum('bqhad,bkhad->bhaqk', qb, k).astype(jnp.float32) * scale
        p = jax.nn.softmax(s, axis=-1)
        a = (p[:, :, 0] - lam * p[:, :, 1]).astype(v.dtype)
        return jnp.einsum('bhqk,bkhe->bqhe', a, v)

    out = lax.map(one_block, _to_blocks(q))
    out = jnp.moveaxis(out, 0, 1)
    return out.reshape(out.shape[0], out.shape[1] * out.shape[2], *out.shape[3:])


def _even_mixer(hx, hc, w_in, w_out, qn_a, kn_a, qn_b, kn_b, sink_a, cos, sin, with_ctx):
    d = HEAD_DIM
    cuts = [int(v) for v in np.cumsum([HEADS_A * d, KV_A * d, KV_A * d, HEADS_B * d, KV_B * d])]

    def heads(p, use_rope):
        b, n = p.shape[:2]
        qa, ka, va, qb, kb, vb = jnp.split(p, cuts, axis=-1)
        qa = _rms(qa.reshape(b, n, HEADS_A, d), qn_a)
        ka = _rms(ka.reshape(b, n, KV_A, d), kn_a)
        qb = _rms(qb.reshape(b, n, HEADS_B, d), qn_b)
        kb = _rms(kb.reshape(b, n, KV_B, d), kn_b)
        if use_rope:
            qa, ka, qb, kb = _rope(qa, cos, sin), _rope(ka, cos, sin), _rope(qb, cos, sin), _rope(kb, cos, sin)
        return (qa.reshape(b, n, KV_A, GQA_GROUP, d), ka, va.reshape(b, n, KV_A, d),
                qb.reshape(b, n, KV_B, GQA_GROUP, d), kb, vb.reshape(b, n, KV_B, d))

    qa_x, ka_x, va_x, qb_x, kb_x, vb_x = heads(hx @ w_in, True)
    qa_c, ka_c, va_c, qb_c, kb_c, vb_c = heads(hc @ w_in, False)
    sink = sink_a.reshape(KV_A, GQA_GROUP)
    oa_x = _window_attend(qa_x, ka_x, va_x, ka_c, va_c, sink)
    ob_x = _dense_attend(qb_x, jnp.concatenate([kb_c, kb_x], axis=1),
                         jnp.concatenate([vb_c, vb_x], axis=1), None)
    out_x = jnp.concatenate([oa_x, ob_x], axis=-1) @ w_out
    out_c = None
    if with_ctx:
        oa_c = _dense_attend(qa_c, ka_c, va_c, sink)
        ob_c = _dense_attend(qb_c, kb_c, vb_c, None)
        out_c = jnp.concatenate([oa_c, ob_c], axis=-1) @ w_out
    return out_x, out_c


def _odd_mixer(hx, hc, w_in, w_out, qn, kn, lq1, lk1, lq2, lk2, subln_w, lam_init, cos, sin, with_ctx):
    d = HEAD_DIM
    qs = 2 * HEADS_C * d

    def heads(p, use_rope):
        b, n = p.shape[:2]
        q, k, v = jnp.split(p, [qs, 2 * qs], axis=-1)
        q = _rms(q.reshape(b, n, HEADS_C, 2, d), qn)
        k = _rms(k.reshape(b, n, HEADS_C, 2, d), kn)
        if use_rope:
            q, k = _rope(q, cos, sin), _rope(k, cos, sin)
        return q, k, v.reshape(b, n, HEADS_C, 2 * d)

    q_x, k_x, v_x = heads(hx @ w_in, True)
    q_c, k_c, v_c = heads(hc @ w_in, False)
    f32 = jnp.float32
    lam = (jnp.exp(jnp.sum(lq1.astype(f32) * lk1.astype(f32)))
           - jnp.exp(jnp.sum(lq2.astype(f32) * lk2.astype(f32))) + lam_init)

    def finish(o):
        o = _rms(o, subln_w) * (1.0 - lam_init)
        return o.reshape(o.shape[0], o.shape[1], -1) @ w_out

    o_x = _diff_attend(q_x, jnp.concatenate([k_c, k_x], axis=1), jnp.concatenate([v_c, v_x], axis=1), lam)
    out_x = finish(o_x)
    out_c = None
    if with_ctx:
        out_c = finish(_diff_attend(q_c, k_c, v_c, lam))
    return out_x, out_c


def _swiglu(h, w_in, w_out):
    gate, up = jnp.split(h @ w_in, 2, axis=-1)
    return (jax.nn.silu(gate) * up) @ w_out


def setup_inputs(seed: int = 0) -> dict:
    key = jax.random.key(seed)
    ks = jax.random.split(key, 26)
    f32 = jnp.float32

    def nrm(k, shape, s):
        return jax.random.normal(k, shape, f32) * s

    def gain(k, shape):
        return 1.0 + 0.02 * jax.random.normal(k, shape, f32)

    D, F, d = D_MODEL, FFN_HIDDEN, HEAD_DIM
    return {
        "x": nrm(ks[0], (BATCH, SEQ, D), 1.0),
        "c": nrm(ks[1], (BATCH, D), 1.0),
        "ctx": nrm(ks[2], (BATCH, CTX_LEN, D), 1.0),
        "c_ctx": nrm(ks[3], (D,), 1.0),
        "mod_w": nrm(ks[4], (DEPTH, D, 6 * D), 0.5 * D ** -0.5),
        "mod_b": nrm(ks[5], (DEPTH, 6 * D), 0.02),
        "norm_mix_w": gain(ks[6], (DEPTH, D)),
        "norm_ffn_w": gain(ks[7], (DEPTH, D)),
        "ev_w_in": nrm(ks[8], (N_EVEN, D, EVEN_IN), D ** -0.5),
        "ev_w_out": nrm(ks[9], (N_EVEN, EVEN_OUT, D), EVEN_OUT ** -0.5),
        "ev_qn_a": gain(ks[10], (N_EVEN, d)),
        "ev_kn_a": gain(ks[11], (N_EVEN, d)),
        "ev_qn_b": gain(ks[12], (N_EVEN, d)),
        "ev_kn_b": gain(ks[13], (N_EVEN, d)),
        "ev_sink_a": nrm(ks[14], (N_EVEN, HEADS_A), 0.5),
        "od_w_in": nrm(ks[15], (N_ODD, D, ODD_IN), D ** -0.5),
        "od_w_out": nrm(ks[16], (N_ODD, ODD_OUT, D), ODD_OUT ** -0.5),
        "od_qn": gain(ks[17], (N_ODD, d)),
        "od_kn": gain(ks[18], (N_ODD, d)),
        "od_lq1": nrm(ks[19], (N_ODD, d), 0.1),
        "od_lk1": nrm(ks[20], (N_ODD, d), 0.1),
        "od_lq2": nrm(ks[21], (N_ODD, d), 0.1),
        "od_lk2": nrm(ks[22], (N_ODD, d), 0.1),
        "od_subln": gain(ks[23], (N_ODD, 2 * d)),
        "ffn_w_in": nrm(ks[24], (DEPTH, D, 2 * F), D ** -0.5),
        "ffn_w_out": nrm(ks[25], (DEPTH, F, D), F ** -0.5),
    }


def reference(x, c, ctx, c_ctx, mod_w, mod_b, norm_mix_w, norm_ffn_w,
              ev_w_in, ev_w_out, ev_qn_a, ev_kn_a, ev_qn_b, ev_kn_b, ev_sink_a,
              od_w_in, od_w_out, od_qn, od_kn, od_lq1, od_lk1, od_lq2, od_lk2, od_subln,
              ffn_w_in, ffn_w_out):
    n_lat = x.shape[1]
    ROWS = n_lat // GRID_W
    cos, sin = _axial_tables(ROWS)
    s_c = jax.nn.silu(c)
    s_cc = jax.nn.silu(c_ctx)[None, :]
    for l in range(DEPTH):
        last = l == DEPTH - 1
        i = l // 2
        mx = (s_c @ mod_w[l] + mod_b[l])[:, None, :]
        mc = (s_cc @ mod_w[l] + mod_b[l])[:, None, :]
        shx1, scx1, gx1, shx2, scx2, gx2 = jnp.split(mx, 6, axis=-1)
        shc1, scc1, gc1, shc2, scc2, gc2 = jnp.split(mc, 6, axis=-1)
        hx = _rms(x, norm_mix_w[l]) * (1.0 + scx1) + shx1
        hc = _rms(ctx, norm_mix_w[l]) * (1.0 + scc1) + shc1
        if l % 2 == 0:
            ax, ac = _even_mixer(hx, hc, ev_w_in[i], ev_w_out[i], ev_qn_a[i], ev_kn_a[i],
                                 ev_qn_b[i], ev_kn_b[i], ev_sink_a[i], cos, sin, not last)
        else:
            lam_init = 0.8 - 0.6 * math.exp(-0.3 * l)
            ax, ac = _odd_mixer(hx, hc, od_w_in[i], od_w_out[i], od_qn[i], od_kn[i],
                                od_lq1[i], od_lk1[i], od_lq2[i], od_lk2[i], od_subln[i],
                                lam_init, cos, sin, not last)
        x = x + gx1 * ax
        x = x + gx2 * _swiglu(_rms(x, norm_ffn_w[l]) * (1.0 + scx2) + shx2, ffn_w_in[l], ffn_w_out[l])
        if not last:
            ctx = ctx + gc1 * ac
            ctx = ctx + gc2 * _swiglu(_rms(ctx, norm_ffn_w[l]) * (1.0 + scc2) + shc2, ffn_w_in[l], ffn_w_out[l])
    return x
```

```python
import math
from contextlib import ExitStack

import numpy as np
import concourse.bass as bass
import concourse.mybir as mybir
from concourse.bass_utils import run_bass_kernel_spmd

F32 = mybir.dt.float32
BF16 = mybir.dt.bfloat16
AF = mybir.ActivationFunctionType
ALU = mybir.AluOpType
AX = mybir.AxisListType

D = 1024
NB = 32
FH = 2816
NJ = 22
EPS = 1e-6
NEG = -30000.0
SCALE = 0.125

SAME_ENGINE_SYNC = True
COMPUTE = ("pe", "act", "dve", "pool")


class Res:
    __slots__ = ("name", "writers", "readers", "dma_cnt", "sem")

    def __init__(self, name):
        self.name = name
        self.writers = []
        self.readers = []
        self.dma_cnt = 0
        self.sem = None


class Op:
    __slots__ = ("eng", "fn", "is_dma", "deps", "signal", "sig_val", "dma_res", "dma_val", "dma_inc")

    def __init__(self, eng, fn, is_dma):
        self.eng = eng
        self.fn = fn
        self.is_dma = is_dma
        self.deps = []
        self.signal = False
        self.sig_val = 0
        self.dma_res = None
        self.dma_val = 0
        self.dma_inc = 16


class Sched:
    def __init__(self):
        self.ops = {e: [] for e in ("pe", "act", "dve", "pool", "sp")}
        self.res = {}
        self.overlap = {}

    def add_overlap(self, a_names, b_names):
        for a in a_names:
            for b in b_names:
                self.overlap.setdefault(a, set()).add(b)
                self.overlap.setdefault(b, set()).add(a)

    def R(self, name):
        r = self.res.get(name)
        if r is None:
            r = self.res[name] = Res(name)
        return r

    def op(self, eng, fn, reads=(), writes=(), accs=(), dma=False, inc=16):
        o = Op(eng, fn, dma)
        o.dma_inc = inc
        deps = []
        ov = self.overlap
        rl = [self.R(r) for r in reads]
        wl = [self.R(r) for r in writes]
        al = [self.R(r) for r in accs]
        ra = [self.R(y) for x in reads for y in ov.get(x, ())]
        wa = [self.R(y) for x in list(writes) + list(accs) for y in ov.get(x, ())]
        for r in rl:
            deps.extend(r.writers)
        for r in ra:
            deps.extend(r.writers)
        for r in wl:
            deps.extend(r.writers)
            deps.extend(r.readers)
        for r in al:
            deps.extend(r.readers)
        for r in wa:
            deps.extend(r.writers)
            deps.extend(r.readers)
        if dma:
            assert len(wl) + len(al) == 1
            o.dma_res = (wl + al)[0]
        seen = set()
        for d in deps:
            if id(d) in seen or d is o:
                continue
            seen.add(id(d))
            if (not d.is_dma) and (not dma) and d.eng == eng:
                if eng == "pe" or not SAME_ENGINE_SYNC:
                    continue
            o.deps.append(d)
        for r in rl:
            r.readers.append(o)
        for r in wl:
            r.writers = [o]
            r.readers = []
        for r in al:
            if r.readers:
                r.writers = [o]
                r.readers = []
            else:
                r.writers.append(o)
        if dma:
            o.dma_res.dma_cnt += inc
            o.dma_val = o.dma_res.dma_cnt
        self.ops[eng].append(o)
        return o

    def fence(self, eng, names):
        self.op(eng, lambda e: e.nop(), writes=list(names))

    def finalize(self, nc, es):
        for lst in self.ops.values():
            for o in lst:
                for d in o.deps:
                    if not d.is_dma:
                        d.signal = True
        for e in COMPUTE:
            c = 0
            for o in self.ops[e]:
                if (not o.is_dma) and o.signal:
                    c += 1
                    o.sig_val = c
        sems = {e: es.enter_context(nc.semaphore("s_" + e)) for e in COMPUTE}
        nd = 0
        for r in self.res.values():
            if r.dma_cnt:
                r.sem = es.enter_context(nc.semaphore("d%d" % nd))
                nd += 1
        block = es.enter_context(nc.Block())
        starters = {"pe": block.tensor, "act": block.scalar, "dve": block.vector,
                    "pool": block.gpsimd, "sp": block.sync}
        for e, starter in starters.items():
            lst = self.ops[e]
            if not lst:
                continue

            def body(eng, lst=lst):
                seen = {}
                for o in lst:
                    need = {}
                    for d in o.deps:
                        if d.is_dma:
                            key, sem, val = ("d", id(d.dma_res)), d.dma_res.sem, d.dma_val
                        else:
                            key, sem, val = ("e", d.eng), sems[d.eng], d.sig_val
                        if seen.get(key, 0) >= val:
                            continue
                        if key not in need or need[key][1] < val:
                            need[key] = (sem, val)
                    for key, (sem, val) in need.items():
                        eng.wait_ge(sem, val)
                        seen[key] = val
                    ins = o.fn(eng)
                    if o.is_dma:
                        ins.then_inc(o.dma_res.sem, o.dma_inc)
                    elif o.signal:
                        ins.then_inc(sems[o.eng], 1)

            starter(body)
        return nd


def build(layers, dbgspec=None):
    nc = bass.Bass("TRN2", target_bir_lowering=False)
    S = Sched()
    es = ExitStack()
    L0 = 0 in layers
    L1 = 1 in layers

    def din(name, shape, dt=F32):
        return nc.dram_tensor(name, list(shape), dt, kind="ExternalInput").ap()

    def dout(name, shape, dt=F32):
        return nc.dram_tensor(name, list(shape), dt, kind="ExternalOutput").ap()

    def dscr(name, shape, dt=BF16):
        return nc.dram_tensor(name, list(shape), dt, kind="Internal").ap()

    def sb(name, shape, dt):
        return es.enter_context(nc.sbuf_tensor(name, list(shape), dt))

    def dma(out, in_, reads, write, slow=False, acc=False):
        kw = {"allow_slow_non_contiguous": True} if slow else {}
        if acc:
            return S.op("sp", lambda e: e.dma_start(out=out, in_=in_, **kw), reads=reads, accs=[write], dma=True)
        return S.op("sp", lambda e: e.dma_start(out=out, in_=in_, **kw), reads=reads, writes=[write], dma=True)

    xo = din("xo", [4096, D])
    xoth = din("xoth", [4096, D])
    ctx_in = din("ctx", [256, D])
    cvec = din("cvec", [2, D])
    cs_own = din("cs_own", [4096, 64])
    cs_oth = din("cs_oth", [4096, 64])
    ident_d = din("ident", [128, 128])
    if L0:
        xhalo = din("xhalo", [256, D])
        cs_halo = din("cs_halo", [256, 64])
        wmask_d = din("wmask", [4, 128, 128])
    W = {}
    for l in layers:
        W[l] = dict(
            mod_w=din("mod_w%d" % l, [D, 6 * D]), mod_b=din("mod_b%d" % l, [1, 6 * D]),
            nmw=din("nmw%d" % l, [1, D]), nfw=din("nfw%d" % l, [1, D]),
            w_in=din("w_in%d" % l, [D, 1536 if l == 0 else 3072]),
            w_out=din("w_out%d" % l, [D, D]),
            fw_in=din("fw_in%d" % l, [D, 2 * FH]), fw_out=din("fw_out%d" % l, [FH, D]),
            gv=din("gv%d" % l, [4 if l == 0 else 2, 64]),
            wq_s=dscr("wq_s%d" % l, [128, 8, 1024]),
            wkv_s=dscr("wkv_s%d" % l, [128, 8, 512 if l == 0 else 2048]),
            wout_s=[dscr("wout_s%d_%d" % (l, v), [128, 8, 1024]) for v in range(2)],
            fwin_s=dscr("fwin_s%d" % l, [NJ, 128, 2, 8, 128]),
            fwout_s=[dscr("fwout_s%d_%d" % (l, v), [NJ, 128, D]) for v in range(2)],
        )
        if l == 0:
            W[l]["sink"] = din("sink0", [1, 8])
        else:
            W[l]["lv"] = din("lv1", [4, 64])
            W[l]["subln"] = din("subln1", [1, 128])
    y_out = dout("y", [4096, D])
    fused = L0 and L1
    if fused:
        x1own_t = nc.dram_tensor("x1own", [4096, D], F32)
        x1all_t = nc.dram_tensor("x1all", [8, 1024, D], F32)
        x1own, x1all = x1own_t.ap(), x1all_t.ap()
        ctx1s = dscr("ctx1s", [256, D], F32)
        cs_full = din("cs_full", [8192, 64])
    ctx_out = dout("ctx1", [256, D]) if (L0 and not L1) else None
    if L1:
        kt1 = dscr("kt1", [8, 128, 66 * 128])
        v1 = dscr("v1", [8, 128, 66, 128])
    dbg = {}

    identf = sb("identf", [128, 128], F32)
    identb = sb("identb", [128, 128], BF16)
    onesf = sb("onesf", [128, 128], F32)
    onesb = sb("onesb", [128, 128], BF16)
    eps_t = sb("eps_t", [128, 1], F32)
    junk = sb("junk", [128, 1024], BF16)
    small = sb("small", [128, 128], F32)
    nT = sb("nT", [128, 2, 8], F32)
    MOD = {}
    for l in layers:
        MOD[l] = dict(modT=sb("modT%d" % l, [128, 4, 8, 2], F32), G1=sb("G1_%d" % l, [128, 8, 2], F32),
                      G2=sb("G2_%d" % l, [128, 8, 2], F32), gq=sb("gq%d" % l, [128, 2, 64], F32),
                      gk=sb("gk%d" % l, [128, 2, 64], F32))
    xc = sb("xc", [128, 4, D], F32)
    xs = sb("xs", [128, D], BF16)
    hT = sb("hT", [128, 8, 512], BF16)
    cst = sb("cst", [128, 4, 64], F32)
    ybuf = sb("ybuf", [128, 1024], F32)
    obuf = sb("obuf", [128, 1024], F32)
    tt = sb("tt", [128, 1024], F32)
    t1 = tt[:, 0:512]
    t2 = tt[:, 512:1024]
    qkb = sb("qkb", [128, 1024], BF16)
    wmix = sb("wmix", [128, 8, 2048], BF16)
    ovA = sb("ovA", [128, 11264], BF16)
    QT = ovA[:, 0:4096].rearrange("p (c n) -> p c n", c=8)
    OT = ovA[:, 4096:8192].rearrange("p (c n) -> p c n", c=8)
    PT = ovA[:, 8192:11264].rearrange("p (b m n) -> p b m n", b=3, m=2)
    actT = ovA[:, 0:NJ * 512].rearrange("p (j n) -> p j n", j=NJ)
    OVA = ["QT", "OT", "PT0", "PT1", "PT2", "actT"]
    epi = sb("epi", [128, 4, 512], F32)
    pss = tt[:].bitcast(BF16).rearrange("p (h m n) -> p h m n", h=2, m=2)
    GS = 16
    fwi = sb("fwi", [128, 2, 2, 8, 128], BF16)
    fwo = sb("fwo", [128, 2, 2, D], BF16)
    sg = sb("sg", [128, 2, 256], F32)
    sgf = sg[:].rearrange("p a n -> p (a n)")
    if L0:
        wm = sb("wm", [128, 4, 128], BF16)
        esk = sb("esk", [128, 8], F32)
    if L1:
        neglam = sb("neglam", [128, 1], F32)
        subw = sb("subw", [128, 1], F32)
    KVN = 39936
    kvr = sb("kvr", [128, KVN], BF16)
    stage = kvr[:, 0:16384].bitcast(F32)
    stb = kvr[:, 16384:24576]
    mrow = kvr[0:2, 24576:36864].bitcast(F32)
    Gbc = kvr[:, 36864:38912].bitcast(F32)
    sel = kvr[0:2, 38912:39424].bitcast(F32).rearrange("p (r m) -> p r m", r=2)
    PRO = ["stage", "stb", "stageA", "stageB", "stbA", "stbB", "mrowv", "mrow0", "mrow1", "gbc"] + ["mws%d" % i for i in range(8)]
    if L0:
        KA = kvr[:, 0:4608]
        KB = kvr[:, 4608:13056]
        VAc = kvr[:, 13056:19968].rearrange("p (n d) -> p n d", d=192)
        VBc = kvr[:, 19968:32640].rearrange("p (n d) -> p n d", d=192)
        VA0, VA1 = VAc[:, :, 0:128], VAc[:, :, 64:192]
        VB0, VB1 = VBc[:, :, 0:128], VBc[:, :, 64:192]
    if L1:
        KH = kvr[:, 0:16896].rearrange("p (b n) -> p b n", b=2)
        VH = kvr[:, 16896:33792].rearrange("p (b n d) -> p b n d", b=2, d=128)
        kst = kvr[:, 33792:34816].rearrange("p (c n) -> p c n", c=8)
        vst = kvr[:, 34816:35840]
    MAINKV = ["KA", "KB", "VA0", "VA1", "VB0", "VB1", "KH0", "KH1", "VH0", "VH1", "kst", "vst"]
    ps = es.enter_context(nc.psum_tensor("ps", [128, 4096], F32))

    def bank(i, n=512, p0=0, p1=128):
        return ps[p0:p1, i * 512:i * 512 + n]

    def bankb(i):
        return ps[:, i * 512:(i + 1) * 512].bitcast(BF16)

    def PB(i):
        return "ps%d" % i

    B_chk = [lambda name: None]
    S.add_overlap(["stage"], ["mws%d" % i for i in range(8)] + ["stageA", "stageB"])
    S.add_overlap(["stb"], ["stbA", "stbB"])
    S.add_overlap(["stageA"], ["mws%d" % i for i in range(8)])
    S.add_overlap(["pss0", "pss1"], ["t1", "t2"])
    for _bk in range(4, 8):
        S.add_overlap(["bc%d" % _bk], ["ps%d" % _bk])
    S.add_overlap(PRO, MAINKV)
    S.add_overlap(["actT"], ["QT", "OT", "PT0", "PT1", "PT2"])
    dma(identf[:], ident_d, [], "identf")
    S.op("dve", lambda e: e.tensor_copy(out=identb[:], in_=identf[:]), ["identf"], ["identb"])
    S.op("dve", lambda e: e.memset(onesf[:], 1.0), [], ["onesf"])
    S.op("dve", lambda e: e.memset(onesb[:], 1.0), [], ["onesb"])
    S.op("dve", lambda e: e.memset(eps_t[:], EPS), [], ["eps"])
    S.op("dve", lambda e: e.tensor_copy(out=sel[0:2, :, :],
                                        in_=identf[0:2, 0:2].unsqueeze(2).broadcast_to([2, 2, 128])),
         ["identf"], ["sel"])
    if L0:
        wmv = stage[:, 0:512].rearrange("p (m q) -> p m q", m=4)
        dma(wmv, wmask_d.rearrange("m k q -> k m q"), [], "stage")
        S.op("dve", lambda e: e.tensor_copy(out=wm[:], in_=wmv), ["stage"], ["wm"])

    def emit_mod(l):
        w, M = W[l], MOD[l]
        modT, G1, G2 = M["modT"], M["G1"], M["G2"]
        scT = small[:, 0:16].rearrange("p (k r) -> p k r", r=2)
        for r in range(2):
            dma(scT[:, :, r], cvec[r].rearrange("(k p) -> p k", p=128), ["scT"], "scin%d" % r, slow=True)
        S.op("act", lambda e: e.activation(out=scT, in_=scT, func=AF.Silu), ["scin0", "scin1"], ["scT"])
        for r in range(2):
            dma(mrow[r:r + 1, :], w["mod_b"], ["mrowv"], "mrow%d" % r)
        mws = [stage[:, i * 512:(i + 1) * 512] for i in range(8)]
        it = 0
        for n in range(12):
            for k in range(8):
                buf = it % 8
                it += 1
                dma(mws[buf], w["mod_w"][k * 128:(k + 1) * 128, n * 512:(n + 1) * 512], [], "mws%d" % buf)
                S.op("pe", lambda e, k=k, buf=buf: e.matmul(bank(0, 512, 0, 2), lhsT=scT[:, k, :], rhs=mws[buf],
                                                            start=(k == 0), stop=(k == 7)),
                     ["scT", "mws%d" % buf], [], accs=[PB(0)])
            S.op("dve", lambda e, n=n: e.tensor_tensor(out=mrow[:, n * 512:(n + 1) * 512], in0=bank(0, 512, 0, 2),
                                                       in1=mrow[:, n * 512:(n + 1) * 512], op=ALU.add),
                 [PB(0), "mrow0", "mrow1"], [], accs=["mrowv"])
        pst = bank(1, 64).rearrange("p (v k r) -> p v k r", v=4, k=8)
        for vi, v in enumerate((0, 1, 3, 4)):
            for k in range(8):
                S.op("pe", lambda e, vi=vi, v=v, k=k: e.transpose(
                    out=pst[:, vi, k, :], in_=mrow[:, v * 1024 + k * 128:v * 1024 + (k + 1) * 128],
                    identity=identf[0:2, 0:2]), ["mrowv", "identf"], [], accs=[PB(1)])
        S.op("dve", lambda e: e.tensor_copy(out=modT[:], in_=pst), [PB(1)], ["modT%d" % l])
        dma(nT[:, 0, :], w["nmw"][0].rearrange("(k p) -> p k", p=128), ["G%d" % l], "nT0", slow=True)
        dma(nT[:, 1, :], w["nfw"][0].rearrange("(k p) -> p k", p=128), ["G%d" % l], "nT1", slow=True)
        for gi, (G, vsc) in enumerate(((G1, 1), (G2, 3))):
            S.op("dve", lambda e, G=G, vsc=vsc: e.tensor_scalar(out=G[:], in0=modT[:, vsc], scalar1=1.0,
                                                                scalar2=None, op0=ALU.add),
                 ["modT%d" % l], ["G%d" % l])
            S.op("dve", lambda e, G=G, gi=gi: e.tensor_tensor(
                out=G[:], in0=G[:], in1=nT[:, gi, :].unsqueeze(2).broadcast_to([128, 8, 2]), op=ALU.mult),
                ["G%d" % l, "nT0", "nT1"], ["G%d" % l])
        gq, gk = M["gq"], M["gk"]
        if l == 0:
            dma(gq[:, 0, :], w["gv"][0].partition_broadcast(128), [], "gains%d" % l, acc=True)
            dma(gk[:, 0, :], w["gv"][1].partition_broadcast(128), [], "gains%d" % l, acc=True)
            dma(gq[:, 1, :], w["gv"][2].partition_broadcast(128), [], "gains%d" % l, acc=True)
            dma(gk[:, 1, :], w["gv"][3].partition_broadcast(128), [], "gains%d" % l, acc=True)
            dma(esk[:], w["sink"][0].partition_broadcast(128), [], "eskin")
            S.op("act", lambda e: e.activation(out=esk[:], in_=esk[:], func=AF.Exp), ["eskin"], ["esk"])
        else:
            dma(gq[:, 0, :], w["gv"][0].partition_broadcast(128), [], "gains%d" % l, acc=True)
            dma(gk[:, 0, :], w["gv"][1].partition_broadcast(128), [], "gains%d" % l, acc=True)
            lvt = small[:, 64:128].rearrange("p (a b) -> p a b", a=1)
            lam_init = 0.8 - 0.6 * math.exp(-0.3 * l)
            lq = ybuf[:, 0:256].rearrange("p (a d) -> p a d", a=4)
            dma(lq, w["lv"].partition_broadcast(128), ["ybuf"], "ybuf")
            dd = small[:, 32:36]
            S.op("dve", lambda e: e.tensor_tensor(out=obuf[:, 0:128].rearrange("p (a d) -> p a d", a=2),
                                                  in0=ybuf[:, 0:256].rearrange("p (a t d) -> p a t d", a=2, t=2)[:, :, 0, :],
                                                  in1=ybuf[:, 0:256].rearrange("p (a t d) -> p a t d", a=2, t=2)[:, :, 1, :],
                                                  op=ALU.mult), ["ybuf"], ["obuf"])
            S.op("dve", lambda e: e.tensor_reduce(out=dd[:, 0:2], in_=obuf[:, 0:128].rearrange("p (a d) -> p a d", a=2),
                                                  axis=AX.X, op=ALU.add), ["obuf"], ["dd"])
            S.op("act", lambda e: e.activation(out=dd[:, 0:2], in_=dd[:, 0:2], func=AF.Exp), ["dd"], ["dd"])
            S.op("dve", lambda e: e.scalar_tensor_tensor(out=neglam[:], in0=dd[:, 1:2], scalar=-lam_init,
                                                         in1=dd[:, 0:1], op0=ALU.add, op1=ALU.subtract),
                 ["dd"], ["neglam"])
            dma(subw[:], w["subln"][0].rearrange("(p o) -> p o", o=1), [], "subwin", slow=True)
            S.op("dve", lambda e: e.tensor_scalar(out=subw[:], in0=subw[:], scalar1=(1.0 - lam_init), scalar2=None,
                                                  op0=ALU.mult), ["subwin"], ["subw"])

    def gate_bc(gidx, r):
        for n in range(2):
            S.op("pe", lambda e, n=n: e.matmul(bank(2 + n), lhsT=sel[0:2, r, :],
                                               rhs=mrow[:, gidx * 1024 + n * 512:gidx * 1024 + (n + 1) * 512],
                                               start=True, stop=True), ["mrowv", "sel"], [PB(2 + n)])
            S.op("act", lambda e, n=n: e.copy(out=Gbc[:, n * 512:(n + 1) * 512], in_=bank(2 + n)),
                 [PB(2 + n)], [], accs=["gbc"])

    def emit_weights(l):
        w = W[l]
        SA = [stage[:, 0:4096], stage[:, 4096:8192]]
        SB = [stb[:, 0:4096], stb[:, 4096:8192]]
        SN = ["stageA", "stageB"]
        BN = ["stbA", "stbB"]
        cnt = [0]

        def unit(loads, cast, stores):
            u = cnt[0] % 2
            cnt[0] += 1
            for dstf, src in loads:
                dma(dstf(SA[u]), src, [], SN[u], acc=True)
            cast(SA[u], SB[u], SN[u], BN[u])
            for dst, srcf, name in stores:
                dma(dst, srcf(SB[u]), [BN[u]], name, acc=True)

        def plain_cast(nel):
            def f(sa, sb_, sn, bn):
                S.op("act", lambda e: e.copy(out=sb_[:, 0:nel], in_=sa[:, 0:nel]), [sn], [bn])
            return f

        def v3(ap, k, n):
            return ap[:, 0:k * n].rearrange("p (k n) -> p k n", k=k)

        wr = w["w_in"]
        for kh in range(2):
            rows = slice(kh * 512, (kh + 1) * 512)
            if l == 0:
                def qcast(sa, sb_, sn, bn):
                    s3, d3 = v3(sa, 4, 1024), v3(sb_, 4, 1024)
                    for grp in range(2):
                        for half in range(2):
                            dv = d3[:, :, grp * 512:(grp + 1) * 512].rearrange("p k (j h d) -> p k j h d", j=4, h=2)[:, :, :, half, :]
                            sv = s3[:, :, grp * 512:(grp + 1) * 512].rearrange("p k (h j d) -> p k h j d", h=2, j=4)[:, :, half, :, :]
                            S.op("act", lambda e, dv=dv, sv=sv: e.copy(out=dv, in_=sv), [sn], [], accs=[bn])
                unit([(lambda sa: v3(sa, 4, 1024)[:, :, 0:512], wr[rows, 0:512].rearrange("(k p) n -> p k n", p=128)),
                      (lambda sa: v3(sa, 4, 1024)[:, :, 512:1024], wr[rows, 768:1280].rearrange("(k p) n -> p k n", p=128))],
                     qcast, [(w["wq_s"][:, kh * 4:(kh + 1) * 4, :], lambda sb_: v3(sb_, 4, 1024), "wq_s%d" % l)])
            else:
                unit([(lambda sa: v3(sa, 4, 1024), wr[rows, 0:1024].rearrange("(k p) n -> p k n", p=128))],
                     plain_cast(4096), [(w["wq_s"][:, kh * 4:(kh + 1) * 4, :], lambda sb_: v3(sb_, 4, 1024), "wq_s%d" % l)])
        if l == 0:
            unit([(lambda sa, i=i: v3(sa, 8, 512)[:, :, i * 128:(i + 1) * 128], wr[:, c0:c0 + 128].rearrange("(k p) n -> p k n", p=128))
                  for i, c0 in enumerate((512, 1280, 640, 1408))],
                 plain_cast(4096), [(w["wkv_s"], lambda sb_: v3(sb_, 8, 512), "wkv_s%d" % l)])
        else:
            for part in range(2):
                for kh in range(2):
                    rows = slice(kh * 512, (kh + 1) * 512)
                    c0 = 1024 + part * 1024
                    unit([(lambda sa: v3(sa, 4, 1024), wr[rows, c0:c0 + 1024].rearrange("(k p) n -> p k n", p=128))],
                         plain_cast(4096),
                         [(w["wkv_s"][:, kh * 4:(kh + 1) * 4, part * 1024:(part + 1) * 1024], lambda sb_: v3(sb_, 4, 1024),
                           "wkv_s%d" % l)])

        def gate_cast(nj):
            def f(sa, sb_, sn, bn):
                S.op("dve", lambda e: e.tensor_tensor(out=v3(sb_, nj, 1024), in0=v3(sa, nj, 1024),
                                                      in1=Gbc.unsqueeze(1).broadcast_to([128, nj, 1024]), op=ALU.mult),
                     [sn, "gbc"], [bn])
            return f

        for v in range(2 if l == 0 else 1):
            gate_bc(2, v)
            for ch in range(2):
                if l == 0:
                    loads = []
                    for cc in range(4):
                        c = ch * 4 + cc
                        grp, j = c // 4, c % 4
                        for half in range(2):
                            r0 = grp * 512 + (half * 4 + j) * 64
                            loads.append((lambda sa, cc=cc, half=half: v3(sa, 4, 1024)[half * 64:(half + 1) * 64, cc, :],
                                          w["w_out"][r0:r0 + 64, :]))
                else:
                    loads = [(lambda sa: v3(sa, 4, 1024),
                              w["w_out"][ch * 512:(ch + 1) * 512, :].rearrange("(c p) n -> p c n", p=128))]
                unit(loads, gate_cast(4),
                     [(w["wout_s"][v][:, ch * 4:(ch + 1) * 4, :], lambda sb_: v3(sb_, 4, 1024), "wout_s%d_%d" % (l, v))])
        for k in range(8):
            for gu in range(2):
                unit([(lambda sa: sa[:, 0:FH], w["fw_in"][k * 128:(k + 1) * 128, gu * FH:(gu + 1) * FH])],
                     plain_cast(FH),
                     [(w["fwin_s"][:, :, gu, k, :].rearrange("j p c -> p j c"),
                       lambda sb_: sb_[:, 0:FH].rearrange("p (j c) -> p j c", c=128), "fwin_s%d" % l)])
        for v in range(2 if l == 0 else 1):
            gate_bc(5, v)
            for j0 in range(0, NJ, 4):
                nj = min(4, NJ - j0)
                unit([(lambda sa, nj=nj: v3(sa, nj, 1024),
                       w["fw_out"][j0 * 128:(j0 + nj) * 128, :].rearrange("(j p) n -> p j n", p=128))],
                     gate_cast(nj),
                     [(w["fwout_s"][v][j0:j0 + nj].rearrange("j p n -> p j n"), lambda sb_, nj=nj: v3(sb_, nj, 1024),
                       "fwout_s%d_%d" % (l, v))])

    def front_parts(l, which, s, r, col0):
        M = MOD[l]
        G = M["G1"] if which == 0 else M["G2"]
        shv = 0 if which == 0 else 2
        modT = M["modT"]
        ss = small[:, 40:41]
        sd = small[:, 41:42]
        rstd = small[:, 42:43]
        pT = bankb(0)
        hname = "hT%d" % (col0 // 128)

        def fa():
            S.op("act", lambda e: e.activation(out=junk[:], in_=xc[:, s, :], func=AF.Square, accum_out=ss),
                 ["xc%d" % s], ["junk", "ss"])
            S.op("act", lambda e: e.activation(out=sd, in_=ss, func=AF.Sqrt, bias=eps_t[:, 0:1], scale=1.0 / D),
                 ["ss", "eps"], ["sd"])
            S.op("dve", lambda e: e.reciprocal(out=rstd, in_=sd), ["sd"], ["rstd"])
            S.op("dve", lambda e: e.tensor_scalar(out=xs[:], in0=xc[:, s, :], scalar1=rstd, scalar2=None, op0=ALU.mult),
                 ["xc%d" % s, "rstd"], ["xs"])

        def fT():
            for k in range(8):
                S.op("pe", lambda e, k=k: e.transpose(out=pT[:, k * 128:(k + 1) * 128], in_=xs[:, k * 128:(k + 1) * 128],
                                                      identity=identb[:]), ["xs", "identb"], [], accs=[PB(0)])

        def fb():
            for k in range(8):
                S.op("dve", lambda e, k=k: e.tensor_scalar(out=hT[:, k, col0:col0 + 128], in0=pT[:, k * 128:(k + 1) * 128],
                                                           scalar1=G[:, k, r:r + 1], scalar2=modT[:, shv, k, r:r + 1],
                                                           op0=ALU.mult, op1=ALU.add),
                     [PB(0), "G%d" % l, "modT%d" % l], [], accs=[hname])

        return fa, fT, fb, hname

    def front(l, which, s, r, col0):
        fa, fT, fb, hname = front_parts(l, which, s, r, col0)
        fa()
        fT()
        fb()
        return hname

    def normrope(src, hshape, gain, rope, dst, srcres, dstres, gres, part=0):
        nh = 1
        for a in hshape:
            nh *= a
        n = nh * 64
        ss = small[:, 44:44 + nh]
        sd = small[:, 64:64 + nh]
        rs = small[:, 84:84 + nh]

        def hv(ap):
            if len(hshape) == 1:
                return ap.rearrange("p (h d) -> p h d", d=64)
            return ap.rearrange("p (a b d) -> p a b d", a=hshape[0], d=64)

        if part in (0, 1):
            S.op("act", lambda e: e.activation(out=obuf[:, 0:n], in_=src, func=AF.Square), srcres, ["obuf"])
            S.op("dve", lambda e: e.tensor_reduce(out=ss, in_=obuf[:, 0:n].rearrange("p (h d) -> p h d", d=64),
                                                  axis=AX.X, op=ALU.add), ["obuf"], ["nss"])
        if part == 1:
            return
        S.op("act", lambda e: e.activation(out=sd, in_=ss, func=AF.Sqrt, bias=eps_t[:, 0:1], scale=1.0 / 64),
             ["nss", "eps"], ["nsd"])
        S.op("dve", lambda e: e.reciprocal(out=rs, in_=sd), ["nsd"], ["nrs"])
        S.op("dve", lambda e: e.tensor_tensor(out=hv(ybuf[:, 0:n]), in0=hv(src), in1=gain, op=ALU.mult),
             srcres + [gres], ["ybuf"])
        if rope:
            y4 = ybuf[:, 0:n].rearrange("p (h i t) -> p h i t", i=32, t=2)
            o4 = obuf[:, 0:n].rearrange("p (h i t) -> p h i t", i=32, t=2)
            C = cst[:, rope - 1, 0:32].unsqueeze(1).broadcast_to([128, nh, 32])
            Sn = cst[:, rope - 1, 32:64].unsqueeze(1).broadcast_to([128, nh, 32])
            a1 = t1[:, 0:nh * 32].rearrange("p (h i) -> p h i", i=32)
            a2 = t2[:, 0:nh * 32].rearrange("p (h i) -> p h i", i=32)
            cr = "cst%d" % (rope - 1)
            S.op("dve", lambda e: e.tensor_tensor(out=a1, in0=y4[:, :, :, 0], in1=C, op=ALU.mult), ["ybuf", cr], ["t1"])
            S.op("dve", lambda e: e.tensor_tensor(out=a2, in0=y4[:, :, :, 1], in1=Sn, op=ALU.mult), ["ybuf", cr], ["t2"])
            S.op("dve", lambda e: e.tensor_tensor(out=o4[:, :, :, 0], in0=a1, in1=a2, op=ALU.subtract),
                 ["t1", "t2", "nss"], ["obuf"])
            S.op("dve", lambda e: e.tensor_tensor(out=a1, in0=y4[:, :, :, 0], in1=Sn, op=ALU.mult), ["ybuf", cr], ["t1"])
            S.op("dve", lambda e: e.tensor_tensor(out=a2, in0=y4[:, :, :, 1], in1=C, op=ALU.mult), ["ybuf", cr], ["t2"])
            S.op("dve", lambda e: e.tensor_tensor(out=o4[:, :, :, 1], in0=a1, in1=a2, op=ALU.add),
                 ["t1", "t2"], [], accs=["obuf"])
            fin, finres = obuf, "obuf"
        else:
            fin, finres = ybuf, "ybuf"
        S.op("dve", lambda e: e.tensor_tensor(out=dst.rearrange("p (h d) -> p h d", d=64),
                                              in0=fin[:, 0:n].rearrange("p (h d) -> p h d", d=64),
                                              in1=rs.unsqueeze(2).broadcast_to([128, nh, 64]), op=ALU.mult),
             [finres, "nrs"], [dstres])

    def load_w(dst, src, name, rd):
        dma(dst, src, [rd], name)

    def attn_unit(qT, nq, kch, mode, out, kvres, sink=None, par=None, hook=None, defer=False, blk4=False, hooks=()):
        nk = len(kch)
        A0, A1 = (4, 5) if par is None else (4 + 2 * par, 5 + 2 * par)
        if par is None:
            bc0 = bank(6, nq, 0, 64)
            bc1f = bank(7, nq)
            bc1 = bank(7, nq, 64, 128)
            bcn0, bcn1 = PB(6), PB(7)
            ei = 0
        else:
            assert nq <= 256 and mode == "aug"
            bc0 = ps[0:64, A0 * 512 + 256:A0 * 512 + 256 + nq]
            bc1f = ps[:, A1 * 512 + 256:A1 * 512 + 256 + nq]
            bc1 = ps[64:128, A1 * 512 + 256:A1 * 512 + 256 + nq]
            bcn0, bcn1 = "bc%d" % A0, "bc%d" % A1
            ei = 2 * par
        sfx = "" if par is None else "_%d" % par

        def Sv(s):
            return ps[:, (2 * s) * 512:(2 * s + 2) * 512].rearrange("p (m n) -> p m n", m=2)[:, :, 0:nq]

        def qk(kc):
            s = kc % 2
            kT, _, _, mask = kch[kc]
            for m in range(2):
                lo = m * 64
                S.op("pe", lambda e, m=m, lo=lo, s=s: e.matmul(bank(2 * s + m, nq), lhsT=kT[lo:lo + 64, :],
                                                               rhs=qT[lo:lo + 64], start=True, stop=(mask is None)),
                     ["QT"] + kvres, [PB(2 * s + m)])
                if mask is not None:
                    mrhs = mask.unsqueeze(1).broadcast_to([128, 4, 128]) if blk4 else mask
                    S.op("pe", lambda e, m=m, s=s, mrhs=mrhs: e.matmul(bank(2 * s + m, nq), lhsT=identb[:], rhs=mrhs,
                                                                       start=False, stop=True),
                         ["identb", "wm"], [], accs=[PB(2 * s + m)])

        qk(0)
        if nk > 1:
            qk(1)
        pend = []
        for kc in range(nk):
            if hook is not None and kc == min(3, nk - 1):
                hook()
            for hk, hf in hooks:
                if kc == min(hk, nk - 1):
                    hf()
            s, b = kc % 2, kc % 3
            _, v0, v1, _ = kch[kc]
            S.op("act", lambda e, s=s, b=b: e.activation(out=PT[:, b, :, 0:nq], in_=Sv(s), func=AF.Exp, scale=SCALE),
                 [PB(2 * s), PB(2 * s + 1)], ["PT%d" % b])
            if kc + 2 < nk:
                qk(kc + 2)
            st, sp_ = (kc == 0), (kc == nk - 1)
            if mode == "aug":
                S.op("pe", lambda e, b=b, v0=v0, st=st, sp_=sp_: e.matmul(bank(A0, nq), lhsT=v0, rhs=PT[:, b, 0, 0:nq],
                                                                          start=st, stop=sp_),
                     ["PT%d" % b] + kvres, [PB(A0)] if st else [], accs=[] if st else [PB(A0)])
                S.op("pe", lambda e, b=b, v1=v1, st=st, sp_=sp_: e.matmul(bank(A1, nq), lhsT=v1, rhs=PT[:, b, 1, 0:nq],
                                                                          start=st, stop=sp_),
                     ["PT%d" % b] + kvres, [PB(A1)] if st else [], accs=[] if st else [PB(A1)])
            else:
                for m in range(2):
                    S.op("pe", lambda e, b=b, m=m, v0=v0, st=st, sp_=sp_: e.matmul(bank(4 + m, nq), lhsT=v0,
                                                                                  rhs=PT[:, b, m, 0:nq], start=st, stop=sp_),
                         ["PT%d" % b] + kvres, [PB(4 + m)] if st else [], accs=[] if st else [PB(4 + m)])
                g0 = (kc // GS) * GS
                gsz = min(GS, nk - g0)
                p = kc - g0
                if p % 2 == 1:
                    half = 0 if p == 1 else 1
                    bprev = (kc - 1) % 3
                    S.op("dve", lambda e, b=b, bprev=bprev, half=half: e.tensor_tensor(
                        out=pss[:, half, :, 0:nq], in0=PT[:, bprev, :, 0:nq], in1=PT[:, b, :, 0:nq], op=ALU.add),
                        ["PT%d" % b, "PT%d" % bprev], ["pss%d" % half])
                    if half == 1:
                        S.op("dve", lambda e: e.tensor_tensor(out=pss[:, 0, :, 0:nq], in0=pss[:, 0, :, 0:nq],
                                                              in1=pss[:, 1, :, 0:nq], op=ALU.add), ["pss0", "pss1"], ["pss0"])
                if p == gsz - 1:
                    if gsz == 1:
                        srcf, sres = (lambda m, b=b: PT[:, b, m, 0:nq]), ["PT%d" % b]
                    else:
                        if gsz % 2 == 1:
                            S.op("dve", lambda e, b=b: e.tensor_tensor(out=pss[:, 0, :, 0:nq], in0=pss[:, 0, :, 0:nq],
                                                                       in1=PT[:, b, :, 0:nq], op=ALU.add),
                                 ["pss0", "PT%d" % b], ["pss0"])
                        srcf, sres = (lambda m: pss[:, 0, m, 0:nq]), ["pss0"]
                    gst, gsp = (g0 == 0), (g0 + gsz == nk)
                    pend.append((srcf, sres, gst, gsp, kc))
                while pend and (pend[0][4] < kc or kc == nk - 1):
                    srcf, sres, gst, gsp, _ = pend.pop(0)
                    for m in range(2):
                        S.op("pe", lambda e, m=m, srcf=srcf, gst=gst, gsp=gsp: e.matmul(bank(6 + m, nq), lhsT=onesb[:],
                                                                                        rhs=srcf(m), start=gst, stop=gsp),
                             sres + ["onesb"], [PB(6 + m)] if gst else [], accs=[] if gst else [PB(6 + m)])
        e2_only = [True]

        def epilogue():
            if mode == "aug":
                rr = epi[:, ei, 0:nq]
                bcs = epi[:, ei + 1, 0:nq]
                for m, (row, bk) in enumerate(((64, A0), (0, A1))):
                    if sink is not None and blk4:
                        S.op("dve", lambda e, m=m, row=row, bk=bk: e.tensor_tensor(
                            out=rr[row:row + 1, :].rearrange("p (j q) -> p j q", j=4),
                            in0=bank(bk, nq, row, row + 1).rearrange("p (j q) -> p j q", j=4),
                            in1=esk[row:row + 1, 4 * m:4 * m + 4].unsqueeze(2).broadcast_to([1, 4, 128]), op=ALU.add),
                            [PB(bk), "esk"], ["rr%d" % m + sfx])
                        S.op("act", lambda e, row=row: e.activation(out=rr[row:row + 1, :], in_=rr[row:row + 1, :], func=AF.Ln),
                             ["rr%d" % m + sfx], ["rr%d" % m + sfx])
                    elif sink is not None:
                        S.op("dve", lambda e, m=m, row=row, bk=bk: e.tensor_scalar(
                            out=rr[row:row + 1, :], in0=bank(bk, nq, row, row + 1),
                            scalar1=esk[row:row + 1, sink[m]:sink[m] + 1], scalar2=None, op0=ALU.add),
                            [PB(bk), "esk"], ["rr%d" % m + sfx])
                        S.op("act", lambda e, row=row: e.activation(out=rr[row:row + 1, :], in_=rr[row:row + 1, :], func=AF.Ln),
                             ["rr%d" % m + sfx], ["rr%d" % m + sfx])
                    else:
                        S.op("act", lambda e, row=row, bk=bk: e.activation(out=rr[row:row + 1, :], in_=bank(bk, nq, row, row + 1),
                                                                           func=AF.Ln), [PB(bk)], ["rr%d" % m + sfx])
                    S.op("act", lambda e, row=row: e.activation(out=rr[row:row + 1, :], in_=rr[row:row + 1, :], func=AF.Exp,
                                                                scale=-1.0), ["rr%d" % m + sfx], ["rr%d" % m + sfx])
                S.op("pe", lambda e: e.matmul(bc0, lhsT=onesf[64:65, 0:64], rhs=rr[64:65, :],
                                              start=True, stop=True), ["rr0" + sfx, "onesf"], [bcn0])
                S.op("pe", lambda e: e.matmul(bc1f, lhsT=onesf[0:1, :], rhs=rr[0:1, :], start=True, stop=True),
                     ["rr1" + sfx, "onesf"], [bcn1])
                S.op("act", lambda e: e.copy(out=bcs[0:64, :], in_=bc0), [bcn0], ["bcs0" + sfx])
                S.op("act", lambda e: e.copy(out=bcs[64:128, :], in_=bc1), [bcn1], ["bcs1" + sfx])
                def v4(ap):
                    return ap.rearrange("p (j q) -> p j q", j=4) if blk4 else ap

                S.op("dve", lambda e: e.tensor_tensor(out=out[0:64], in0=v4(bank(A0, nq, 0, 64)), in1=v4(bcs[0:64, :]),
                                                      op=ALU.mult), [PB(A0), "bcs0" + sfx], [], accs=["OT"])
                S.op("dve", lambda e: e.tensor_tensor(out=out[64:128], in0=v4(bank(A1, nq, 64, 128)), in1=v4(bcs[64:128, :]),
                                                      op=ALU.mult), [PB(A1), "bcs1" + sfx], [], accs=["OT"])
            else:
                diff_e1()
                diff_e2()

        r0, r1, o0, o1 = (epi[:, i, 0:nq] for i in range(4))

        def diff_copy():
            S.op("dve", lambda e: e.tensor_copy(out=o0, in_=bank(4, nq)), [PB(4)], ["e_o0"])
            S.op("dve", lambda e: e.tensor_copy(out=o1, in_=bank(5, nq)), [PB(5)], ["e_o1"])

        def diff_e1():
            for m, rbuf in enumerate((r0, r1)):
                S.op("act", lambda e, m=m, rbuf=rbuf: e.activation(out=rbuf, in_=bank(6 + m, nq), func=AF.Ln),
                     [PB(6 + m)], ["e_r%d" % m])
                S.op("act", lambda e, rbuf=rbuf: e.activation(out=rbuf, in_=rbuf, func=AF.Exp, scale=-1.0),
                     ["e_r%d" % m], ["e_r%d" % m])
            S.op("dve", lambda e: e.tensor_tensor(out=o0, in0=o0, in1=r0, op=ALU.mult), ["e_o0", "e_r0"], ["e_o0"])
            S.op("dve", lambda e: e.tensor_tensor(out=o1, in0=o1, in1=r1, op=ALU.mult), ["e_o1", "e_r1"], ["e_o1"])
            S.op("dve", lambda e: e.scalar_tensor_tensor(out=o0, in0=o1, scalar=neglam[:, 0:1], in1=o0,
                                                         op0=ALU.mult, op1=ALU.add), ["e_o0", "e_o1", "neglam"], ["e_o0"])
            S.op("dve", lambda e: e.tensor_tensor(out=r0, in0=o0, in1=o0, op=ALU.mult), ["e_o0"], ["e_r0"])

        def diff_e2():
            S.op("pe", lambda e: e.matmul(bank(6, nq), lhsT=onesf[:], rhs=r0, start=True, stop=True),
                 ["e_r0", "onesf"], [PB(6)])
            S.op("act", lambda e: e.activation(out=r1, in_=bank(6, nq), func=AF.Ln, bias=eps_t[:, 0:1], scale=1.0 / 128),
                 [PB(6), "eps"], ["e_r1"])
            S.op("act", lambda e: e.activation(out=r1, in_=r1, func=AF.Exp, scale=-0.5), ["e_r1"], ["e_r1"])
            S.op("dve", lambda e: e.scalar_tensor_tensor(out=out, in0=o0, scalar=subw[:, 0:1], in1=r1,
                                                         op0=ALU.mult, op1=ALU.mult), ["e_o0", "e_r1", "subw"], [],
                 accs=["OT"])

        if mode == "diff":
            diff_copy()
            if defer:
                return diff_e1, diff_e2
            diff_e1()
            diff_e2()
            return None
        if defer:
            e2_only[0] = None
            epilogue()
            return epilogue
        if par is None:
            epilogue()
            return None
        return epilogue

    SRC_RD = [[]]

    def load_block(src, s, csrc, csl):
        dma(xc[:, s, :], src, SRC_RD[0], "xc%d" % s)
        if csrc is not None:
            dma(cst[:, csl, :], csrc, [], "cst%d" % csl)

    def pass_a(l, blocks):
        w, M = W[l], MOD[l]
        nkv = 512 if l == 0 else 2048
        load_w(wmix[:, :, 0:nkv], w["wkv_s"], "wmix", "wkv_s%d" % l)
        n = len(blocks)
        hnames = {}

        def ld(j):
            if j < n:
                load_block(blocks[j][0], j % 4, blocks[j][1], j % 4)

        fparts = {}

        def fr_a(j):
            if j < n:
                fparts[j] = front_parts(l, 0, j % 4, blocks[j][2], (j % 4) * 128)
                hnames[j] = fparts[j][3]
                fparts[j][0]()

        def fr_T(j):
            if j < n:
                fparts[j][1]()

        def fr_b(j):
            if j < n:
                fparts[j][2]()

        def fr(j):
            fr_a(j)
            fr_T(j)
            fr_b(j)

        def kbanks(j):
            if l == 0:
                return [1] if j % 2 == 0 else [3]
            return [1, 2] if j % 2 == 0 else [5, 6]

        def proj(j, part):
            if j >= n:
                return
            hc = (j % 4) * 128
            hn = hnames[j]
            if l == 0:
                bks, c0 = kbanks(j), 0
            elif part == 0:
                bks, c0 = kbanks(j), 0
            else:
                bks, c0 = [3, 4], 1024
            for bi, bk in enumerate(bks):
                for k in range(8):
                    S.op("pe", lambda e, k=k, bk=bk, bi=bi, hc=hc, c0=c0: e.matmul(
                        bank(bk), lhsT=hT[:, k, hc:hc + 128], rhs=wmix[:, k, c0 + bi * 512:c0 + (bi + 1) * 512],
                        start=(k == 0), stop=(k == 7)),
                        [hn, "wmix"], [PB(bk)] if k == 0 else [], accs=[] if k == 0 else [PB(bk)])

        def post_v(j):
            ib = blocks[j][4]
            S.op("act", lambda e: e.copy(out=vst, in_=ps[:, 1536:2560]), [PB(3), PB(4)], ["vst"])
            dma(v1[:, :, ib, :].rearrange("h p d -> p h d"), vst.rearrange("p (h d) -> p h d", d=128), ["vst"], "v1",
                acc=True)

        def post_k1(j, part):
            if j >= n:
                return
            src, csrc, r, ia, ib = blocks[j]
            rope = (j % 4) + 1 if csrc is not None else 0
            bks = kbanks(j)
            if l == 0:
                gain = M["gk"][:].unsqueeze(2).broadcast_to([128, 2, 2, 64])
                kb_ = bks[0]
                if ia is not None and part == 1:
                    S.op("act", lambda e: e.copy(out=VAc[:, ia, 0:64], in_=bank(kb_)[:, 256:320]), [PB(kb_)], [], accs=["VA0"])
                    S.op("act", lambda e: e.copy(out=VAc[:, ia, 128:192], in_=bank(kb_)[:, 320:384]), [PB(kb_)], [], accs=["VA1"])
                if ib is not None and part == 1:
                    S.op("act", lambda e: e.copy(out=VBc[:, ib, 0:64], in_=bank(kb_)[:, 384:448]), [PB(kb_)], [], accs=["VB0"])
                    S.op("act", lambda e: e.copy(out=VBc[:, ib, 128:192], in_=bank(kb_)[:, 448:512]), [PB(kb_)], [], accs=["VB1"])
                normrope(bank(bks[0], 256), (2, 2), gain, rope, qkb[:, 0:256], [PB(bks[0])], "qkb", "gains%d" % l, part=part)
            else:
                gain = M["gk"][:, 0:1, :].broadcast_to([128, 16, 64])
                srcap = ps[:, bks[0] * 512:bks[0] * 512 + 1024]
                normrope(srcap, (16,), gain, rope, qkb[:, 0:1024], [PB(bks[0]), PB(bks[1])], "qkb", "gains%d" % l, part=part)

        def post_k2(j, part):
            if j < 0 or j >= n:
                return
            src, csrc, r, ia, ib = blocks[j]
            if l == 0:
                pT = bankb(2)
                if part == 0:
                    for t in range(2):
                        S.op("pe", lambda e, t=t: e.transpose(out=pT[:, t * 128:(t + 1) * 128], in_=qkb[:, t * 128:(t + 1) * 128],
                                                              identity=identb[:]), ["qkb", "identb"], [], accs=[PB(2)])
                    return
                if ia is not None:
                    S.op("act", lambda e: e.copy(out=KA[:, ia * 128:(ia + 1) * 128], in_=pT[:, 0:128]), [PB(2)], [], accs=["KA"])
                if ib is not None:
                    S.op("act", lambda e: e.copy(out=KB[:, ib * 128:(ib + 1) * 128], in_=pT[:, 128:256]), [PB(2)], [], accs=["KB"])
            else:
                pT = bankb(7)
                if part == 0:
                    for t in range(8):
                        S.op("pe", lambda e, t=t: e.transpose(out=pT[:, t * 128:(t + 1) * 128], in_=qkb[:, t * 128:(t + 1) * 128],
                                                              identity=identb[:]), ["qkb", "identb"], [], accs=[PB(7)])
                    return
                S.op("act", lambda e: e.copy(out=kst.rearrange("p c n -> p (c n)"), in_=pT), [PB(7)], ["kst"])
                dma(kt1[:, :, ib * 128:(ib + 1) * 128].rearrange("h p n -> p h n"), kst, ["kst"], "kt1", acc=True)

        for j in range(3):
            ld(j)
        fr(0)
        proj(0, 0)
        proj(0, 1) if l == 1 else None
        fr(1)
        for i in range(n):
            ld(i + 3)
            post_k1(i, 1)
            fr_a(i + 2)
            post_k2(i - 1, 0)
            proj(i + 1, 0)
            fr_T(i + 2)
            if l == 1:
                post_v(i)
                proj(i + 1, 1)
            post_k1(i, 2)
            fr_b(i + 2)
            post_k2(i - 1, 1)
        post_k2(n - 1, 0)
        post_k2(n - 1, 1)

    def pass_b_chunk(l, srcs, css, r, own0, outs, is_ctx, first):
        w, M = W[l], MOD[l]
        nblk = len(srcs)
        nq = nblk * 128
        for i in range(nblk):
            load_block(srcs[i], i, css[i] if css else None, i)
        hq = {}
        qparts = {}

        def qfr_a(j):
            if j < nblk:
                qparts[j] = front_parts(l, 0, j, r, j * 128)
                hq[j] = qparts[j][3]
                qparts[j][0]()

        def qfr_T(j):
            if j < nblk:
                qparts[j][1]()

        def qfr_b(j):
            if j < nblk:
                qparts[j][2]()

        def qproj(j):
            if j >= nblk:
                return
            bks = (1, 2) if j % 2 == 0 else (3, 4)
            for nn in range(2):
                for k in range(8):
                    S.op("pe", lambda e, k=k, nn=nn, j=j, bk=bks[nn]: e.matmul(
                        bank(bk), lhsT=hT[:, k, j * 128:(j + 1) * 128], rhs=wmix[:, k, nn * 512:(nn + 1) * 512],
                        start=(k == 0), stop=(k == 7)),
                        [hq[j], "wmix"], [PB(bks[nn])] if k == 0 else [], accs=[] if k == 0 else [PB(bks[nn])])

        def qnr(j, part):
            if j >= nblk:
                return
            bks = (1, 2) if j % 2 == 0 else (3, 4)
            if l == 0:
                gain = M["gq"][:].unsqueeze(2).broadcast_to([128, 2, 8, 64])
            else:
                gain = M["gq"][:, 0:1, :].broadcast_to([128, 16, 64])
            rope = j + 1 if css else 0
            normrope(ps[:, bks[0] * 512:bks[0] * 512 + 1024], (2, 8) if l == 0 else (16,), gain, rope, qkb[:, 0:1024],
                     [PB(bks[0]), PB(bks[1])], "qkb", "gains%d" % l, part=part)

        def qT(j, part):
            if j < 0 or j >= nblk:
                return
            pT = bankb(5)
            if part == 0:
                for t in range(8):
                    S.op("pe", lambda e, t=t: e.transpose(out=pT[:, t * 128:(t + 1) * 128], in_=qkb[:, t * 128:(t + 1) * 128],
                                                          identity=identb[:]), ["qkb", "identb"], [], accs=[PB(5)])
            else:
                S.op("act", lambda e, j=j: e.copy(out=QT[:, :, j * 128:(j + 1) * 128],
                                                  in_=pT.rearrange("p (c n) -> p c n", c=8)), [PB(5)], [], accs=["QT"])

        qfr_a(0)
        qfr_T(0)
        qfr_b(0)
        qproj(0)
        qfr_a(1)
        qfr_T(1)
        qfr_b(1)
        for i in range(nblk):
            qnr(i, 1)
            qfr_a(i + 2)
            qT(i - 1, 0)
            qproj(i + 1)
            qfr_T(i + 2)
            qnr(i, 2)
            qfr_b(i + 2)
            qT(i - 1, 1)
        qT(nblk - 1, 0)
        qT(nblk - 1, 1)
        B_chk[0]('qproj')
        if l == 0:
            pend_ep = [None]
            ucnt = [0]

            def unit_p(*a, **kw):
                ep = attn_unit(*a, par=ucnt[0] % 2, **kw)
                ucnt[0] += 1
                if pend_ep[0] is not None:
                    pend_ep[0]()
                pend_ep[0] = ep

            def flush_p():
                if pend_ep[0] is not None:
                    pend_ep[0]()
                    pend_ep[0] = None

            if is_ctx:
                for j in range(4):
                    kch = [(KA[:, (34 + t) * 128:(35 + t) * 128], VA0[:, 34 + t, :], VA1[:, 34 + t, :], None) for t in range(2)]
                    unit_p(QT[:, j, 0:nq], nq, kch, "aug", OT[:, j, 0:nq], ["KA", "VA0", "VA1"], sink=(j, 4 + j))
                for j in range(4):
                    kch = [(KB[:, t * 128:(t + 1) * 128], VB0[:, t, :], VB1[:, t, :], None) for t in range(2)]
                    unit_p(QT[:, 4 + j, 0:nq], nq, kch, "aug", OT[:, 4 + j, 0:nq], ["KB", "VB0", "VB1"])
                flush_p()
            else:
                for i in range(nblk):
                    ib = own0 + i
                    prev = ib - 1 if ib > 0 else 32
                    nxt = ib + 1 if ib < NB - 1 else 33
                    mp = 0 if ib == 0 else 1
                    mn = 3 if ib == NB - 1 else 2
                    order = [(34, None), (35, None), (prev, mp), (ib, None), (nxt, mn)]
                    kch = [(KA[:, t * 128:(t + 1) * 128], VA0[:, t, :], VA1[:, t, :],
                            None if mk is None else wm[:, mk, :]) for t, mk in order]
                    attn_unit(QT[:, 0:4, i * 128:(i + 1) * 128], 512, kch, "aug", OT[:, 0:4, i * 128:(i + 1) * 128],
                              ["KA", "VA0", "VA1"], sink=(0, 4), blk4=True)
                kch = [(KB[:, t * 128:(t + 1) * 128], VB0[:, t, :], VB1[:, t, :], None) for t in range(66)]
                for j in range(4):
                    attn_unit(QT[:, 4 + j, 0:nq], nq, kch, "aug", OT[:, 4 + j, 0:nq], ["KB", "VB0", "VB1"])
        else:
            def loadh(h):
                b = h % 2
                dma(KH[:, b, :], kt1[h], ["kt1"], "KH%d" % b)
                dma(VH[:, b, :, :], v1[h], ["v1"], "VH%d" % b)
            loadh(0)
            pend_e = None
            for h in range(8):
                if h + 1 < 8:
                    loadh(h + 1)
                b = h % 2
                kch = [(KH[:, b, t * 128:(t + 1) * 128], VH[:, b, t, :], None, None) for t in range(66)]
                hk = [] if pend_e is None else [(2, pend_e[0]), (6, pend_e[1])]
                pend_e = attn_unit(QT[:, h, 0:nq], nq, kch, "diff", OT[:, h, 0:nq], ["KH%d" % b, "VH%d" % b],
                                   defer=True, hooks=hk)
            pend_e[0]()
            pend_e[1]()
        B_chk[0]('attn')
        if first:
            load_w(wmix[:, :, 1024:2048], w["wout_s"][r], "wmix2", "wout_s%d_%d" % (l, r))
        for i in range(nblk):
            for nn in range(2):
                for c in range(8):
                    S.op("pe", lambda e, c=c, nn=nn, i=i: e.matmul(bank(1 + nn), lhsT=OT[:, c, i * 128:(i + 1) * 128],
                                                                   rhs=wmix[:, c, 1024 + nn * 512:1024 + (nn + 1) * 512],
                                                                   start=(c == 0), stop=(c == 7)),
                         ["OT", "wmix2"], [PB(1 + nn)] if c == 0 else [], accs=[] if c == 0 else [PB(1 + nn)])
                S.op("dve", lambda e, nn=nn, i=i: e.tensor_tensor(out=xc[:, i, nn * 512:(nn + 1) * 512], in0=bank(1 + nn),
                                                                  in1=xc[:, i, nn * 512:(nn + 1) * 512], op=ALU.add),
                     [PB(1 + nn)], ["xc%d" % i])
        B_chk[0]('oproj')
        fwin_s, fwout_s = w["fwin_s"], w["fwout_s"][r]
        for sub in range(1):
            blks = list(range(nblk))
            ns = len(blks) * 128
            fps = [front_parts(l, 1, i, r, bi * 128) for bi, i in enumerate(blks)]
            for bi in range(len(fps) + 1):
                if bi < len(fps):
                    fps[bi][0]()
                if bi > 0:
                    fps[bi - 1][2]()
                if bi < len(fps):
                    fps[bi][1]()
            hres = ["hT%d" % bi for bi in range(len(blks))]
            dma(fwi[:, 0], fwin_s[0], ["fwin_s%d" % l], "fwi0")
            for j in range(NJ):
                if j + 1 < NJ:
                    dma(fwi[:, (j + 1) % 2], fwin_s[j + 1], ["fwin_s%d" % l], "fwi%d" % ((j + 1) % 2))
                b = j % 2
                for gu in range(2):
                    for k in range(8):
                        S.op("pe", lambda e, k=k, gu=gu, b=b, fb_=1 + 2 * (j % 2): e.matmul(bank(fb_ + gu, ns), lhsT=fwi[:, b, gu, k, :],
                                                                       rhs=hT[:, k, 0:ns], start=(k == 0), stop=(k == 7)),
                             hres + ["fwi%d" % b], [PB(1 + 2 * (j % 2) + gu)] if k == 0 else [],
                             accs=[] if k == 0 else [PB(1 + 2 * (j % 2) + gu)])
                fb_ = 1 + 2 * (j % 2)
                S.op("act", lambda e, fb_=fb_: e.activation(out=sgf[:, 0:ns], in_=bank(fb_, ns), func=AF.Silu), [PB(fb_)], ["sg"])
                S.op("dve", lambda e, j=j, fb_=fb_: e.tensor_tensor(out=actT[:, j, 0:ns], in0=bank(fb_ + 1, ns), in1=sgf[:, 0:ns],
                                                                    op=ALU.mult), [PB(fb_ + 1), "sg"], [], accs=["actT"])
            dma(fwo[:, 0], fwout_s[0:2].rearrange("j p n -> p j n"), ["fwout_s%d_%d" % (l, r)], "fwo0")
            for jp in range(NJ // 2):
                if jp + 1 < NJ // 2:
                    dma(fwo[:, (jp + 1) % 2], fwout_s[2 * (jp + 1):2 * (jp + 1) + 2].rearrange("j p n -> p j n"), ["fwout_s%d_%d" % (l, r)],
                        "fwo%d" % ((jp + 1) % 2))
                b = jp % 2
                for jj in range(2):
                    j = jp * 2 + jj
                    for bi in range(len(blks)):
                        for nn in range(2):
                            bk = bi * 2 + nn
                            S.op("pe", lambda e, j=j, jj=jj, bi=bi, nn=nn, bk=bk, b=b: e.matmul(
                                bank(bk), lhsT=actT[:, j, bi * 128:(bi + 1) * 128], rhs=fwo[:, b, jj, nn * 512:(nn + 1) * 512],
                                start=(j == 0), stop=(j == NJ - 1)),
                                ["actT", "fwo%d" % b], [PB(bk)] if j == 0 else [], accs=[] if j == 0 else [PB(bk)])
            for bi, i in enumerate(blks):
                for nn in range(2):
                    bk = bi * 2 + nn
                    S.op("dve", lambda e, nn=nn, i=i, bk=bk: e.tensor_tensor(out=xc[:, i, nn * 512:(nn + 1) * 512],
                                                                             in0=bank(bk), in1=xc[:, i, nn * 512:(nn + 1) * 512],
                                                                             op=ALU.add), [PB(bk)], ["xc%d" % i])
                dma(outs[i], xc[:, i, :], ["xc%d" % i], "yout%d" % i)

    def blk(t, i):
        return t[i * 128:(i + 1) * 128, :]

    stop = dbgspec.get("stop") if dbgspec else None

    class StopBuild(Exception):
        pass

    def chk(name):
        if stop == name:
            raise StopBuild()

    B_chk[0] = chk
    try:
      for l in layers:
          if stop == "consts":
              break
          emit_mod(l)
          if stop == "mod":
              dma(y_out[0:2, :], mrow[:, 0:1024], ["mrowv"], "yout0")
              dma(y_out[128:256, 0:64], MOD[l]["modT"][:].rearrange("p a b c -> p (a b c)"), ["modT%d" % l], "yout1")
              dma(y_out[256:384, 0:16], MOD[l]["G1"][:].rearrange("p a b -> p (a b)"), ["G%d" % l], "yout2")
              break
          emit_weights(l)
          if stop == "weights":
              break
          if l == 0:
              S.op("pool", lambda e: e.memset(VAc[:, :, 64:128], 0.0), [], ["VA0", "VA1"])
              S.op("pool", lambda e: e.memset(VAc[:, :, 64:65], 1.0), [], ["VA0", "VA1"])
              S.op("pool", lambda e: e.memset(VBc[:, :, 64:128], 0.0), [], ["VB0", "VB1"])
              S.op("pool", lambda e: e.memset(VBc[:, :, 64:65], 1.0), [], ["VB0", "VB1"])
              blocks = [(blk(ctx_in, t), None, 1, 34 + t, t) for t in range(2)]
              blocks += [(blk(xo, t), blk(cs_own, t), 0, t, 2 + t) for t in range(NB)]
              blocks += [(blk(xhalo, t), blk(cs_halo, t), 0, 32 + t, None) for t in range(2)]
              blocks += [(blk(xoth, t), blk(cs_oth, t), 0, None, 34 + t) for t in range(NB)]
              x_src, c_src = xo, ctx_in
              x_dst = x1own if fused else y_out
              c_dst = ctx1s if fused else ctx_out
              SRC_RD[0] = []
          elif fused:
              YO = ["yout0", "yout1", "yout2", "yout3"]
              SRC_RD[0] = YO + ["x1all"]
              blocks = [(blk(ctx1s, t), None, 1, None, t) for t in range(2)]
              for ci in range(8):
                  for rr in range(2):
                      for bl in range(4):
                          pos = rr * 4096 + ci * 512 + bl * 128
                          blocks.append((x1all[ci, rr * 512 + bl * 128:rr * 512 + (bl + 1) * 128, :],
                                         cs_full[pos:pos + 128, :], 0, None, 2 + (ci * 2 + rr) * 4 + bl))
              x_src, c_src, x_dst, c_dst = x1own, ctx1s, y_out, None
          else:
              blocks = [(blk(ctx_in, t), None, 1, None, t) for t in range(2)]
              blocks += [(blk(xo, t), blk(cs_own, t), 0, None, 2 + t) for t in range(NB)]
              blocks += [(blk(xoth, t), blk(cs_oth, t), 0, None, 34 + t) for t in range(NB)]
              x_src, c_src, x_dst, c_dst = xo, ctx_in, y_out, None
              SRC_RD[0] = []
          if dbgspec and dbgspec.get("nblocksA"):
              blocks = blocks[:dbgspec["nblocksA"]]
          pass_a(l, blocks)
          chk("passA")
          load_w(wmix[:, :, 0:1024], W[l]["wq_s"], "wmix", "wq_s%d" % l)
          nchunks = dbgspec.get("nchunks", 8) if dbgspec else 8
          for ci in range(nchunks):
              srcs = [blk(x_src, ci * 4 + i) for i in range(4)]
              css = [blk(cs_own, ci * 4 + i) for i in range(4)]
              outs = [blk(x_dst, ci * 4 + i) for i in range(4)]
              pass_b_chunk(l, srcs, css, 0, ci * 4, outs, False, ci == 0)
              if fused and l == 0:
                  S.op("pool", lambda e, ci=ci: e.collective_compute(
                      "AllGather", ALU.bypass, replica_groups=[[0, 1], [2, 3], [4, 5], [6, 7]],
                      ins=[x1own_t.ap()[ci * 512:(ci + 1) * 512, :].opt()], outs=[x1all_t.ap()[ci].opt()]),
                      ["yout0", "yout1", "yout2", "yout3"], [], accs=["x1all"], dma=True, inc=1)
          if l == 0 and c_dst is not None and not (dbgspec and dbgspec.get("noctx")):
              load_w(wmix[:, :, 1024:2048], W[l]["wout_s"][1], "wmix2", "wout_s%d_1" % l)
              pass_b_chunk(l, [blk(c_src, t) for t in range(2)], None, 1, 0, [blk(c_dst, t) for t in range(2)], True, False)
    except StopBuild:
        pass
    S.op("sp", lambda e: e.nop(), ["yout0", "yout1", "yout2", "yout3"], [])
    nd = S.finalize(nc, es)
    es.close()
    return nc


def _rope_tables(n):
    t = np.arange(n)
    row = (t // 64).astype(np.float32)
    col = (t % 64).astype(np.float32)
    inv = (np.float32(10000.0) ** (-np.arange(16, dtype=np.float32) / np.float32(16))).astype(np.float32)
    ang = np.concatenate([row[:, None] * inv, col[:, None] * inv], axis=-1).astype(np.float32)
    return np.concatenate([np.cos(ang), np.sin(ang)], axis=-1).astype(np.float32)


def _masks(h):
    k = np.arange(128)[:, None]
    q = np.arange(128)[None, :]
    tri_prev = np.where(k >= q, 0.0, NEG).astype(np.float32)
    tri_next = np.where(k <= q, 0.0, NEG).astype(np.float32)
    allneg = np.full((128, 128), NEG, np.float32)
    return np.stack([allneg if h == 0 else tri_prev, tri_prev, tri_next, tri_next if h == 0 else allneg])


_NC_CACHE = {}


def _get_nc(layers):
    key = tuple(layers)
    if key not in _NC_CACHE:
        _NC_CACHE[key] = build(list(layers))
    return _NC_CACHE[key]


def _layer_maps(l, xs_full, ctx_full, inp):
    cs = _rope_tables(8192)
    ident = np.eye(128, dtype=np.float32)
    f = lambda a: np.ascontiguousarray(a, dtype=np.float32)
    maps = []
    for core in range(8):
        b, h = core // 2, core % 2
        o0, t0 = h * 4096, (1 - h) * 4096
        m = {
            "xo": f(xs_full[b, o0:o0 + 4096]), "xoth": f(xs_full[b, t0:t0 + 4096]), "ctx": f(ctx_full[b]),
            "cvec": f(np.stack([inp["c"][b], inp["c_ctx"]])),
            "cs_own": f(cs[o0:o0 + 4096]), "cs_oth": f(cs[t0:t0 + 4096]), "ident": ident,
            "mod_w%d" % l: f(inp["mod_w"][l]), "mod_b%d" % l: f(inp["mod_b"][l][None]),
            "nmw%d" % l: f(inp["norm_mix_w"][l][None]), "nfw%d" % l: f(inp["norm_ffn_w"][l][None]),
            "fw_in%d" % l: f(inp["ffn_w_in"][l]), "fw_out%d" % l: f(inp["ffn_w_out"][l]),
        }
        if l == 0:
            halo = np.zeros((256, D), np.float32)
            csh = np.zeros((256, 64), np.float32)
            if h == 1:
                halo[0:128] = xs_full[b, 4096 - 128:4096]
                csh[0:128] = cs[4096 - 128:4096]
            else:
                halo[128:256] = xs_full[b, 4096:4096 + 128]
                csh[128:256] = cs[4096:4096 + 128]
            m.update({"xhalo": halo, "cs_halo": csh, "wmask": _masks(h),
                      "w_in0": f(inp["ev_w_in"][0]), "w_out0": f(inp["ev_w_out"][0]),
                      "gv0": f(np.stack([inp["ev_qn_a"][0], inp["ev_kn_a"][0], inp["ev_qn_b"][0], inp["ev_kn_b"][0]])),
                      "sink0": f(inp["ev_sink_a"][0][None])})
        else:
            m.update({"w_in1": f(inp["od_w_in"][0]), "w_out1": f(inp["od_w_out"][0]),
                      "gv1": f(np.stack([inp["od_qn"][0], inp["od_kn"][0]])),
                      "lv1": f(np.stack([inp["od_lq1"][0], inp["od_lk1"][0], inp["od_lq2"][0], inp["od_lk2"][0]])),
                      "subln1": f(inp["od_subln"][0][None])})
        maps.append(m)
    return maps


FUSED = True


def kernel(**inputs):
    inp = {k: np.asarray(v) for k, v in inputs.items()}
    x = inp["x"].astype(np.float32, copy=False)
    ctx = inp["ctx"].astype(np.float32, copy=False)
    if FUSED:
        m0 = _layer_maps(0, x, ctx, inp)
        m1 = _layer_maps(1, x, ctx, inp)
        cs = _rope_tables(8192)
        maps = []
        for core in range(8):
            m = dict(m0[core])
            for k, v in m1[core].items():
                if k not in m:
                    m[k] = v
            m["cs_full"] = cs
            maps.append(m)
        res = run_bass_kernel_spmd(_get_nc((0, 1)), maps, core_ids=list(range(8)))
        out = np.stack([np.concatenate([res.results[2 * b]["y"], res.results[2 * b + 1]["y"]], axis=0) for b in range(4)])
        return out.astype(np.float32)
    res0 = run_bass_kernel_spmd(_get_nc((0,)), _layer_maps(0, x, ctx, inp), core_ids=list(range(8)))
    x1 = np.stack([np.concatenate([res0.results[2 * b]["y"], res0.results[2 * b + 1]["y"]], axis=0) for b in range(4)])
    ctx1 = np.stack([res0.results[2 * b]["ctx1"] for b in range(4)])
    res1 = run_bass_kernel_spmd(_get_nc((1,)), _layer_maps(1, x1, ctx1, inp), core_ids=list(range(8)))
    out = np.stack([np.concatenate([res1.results[2 * b]["y"], res1.results[2 * b + 1]["y"]], axis=0) for b in range(4)])
    return out.astype(np.float32)
```

```python
import math
from contextlib import ExitStack

import numpy as np
import concourse.bass as bass
import concourse.mybir as mybir
from concourse.bass_utils import run_bass_kernel_spmd

F32 = mybir.dt.float32
BF16 = mybir.dt.bfloat16
AF = mybir.ActivationFunctionType
ALU = mybir.AluOpType
AX = mybir.AxisListType

D = 1024
NB = 32
FH = 2816
NJ = 22
EPS = 1e-6
NEG = -30000.0
SCALE = 0.125

SAME_ENGINE_SYNC = True
COMPUTE = ("pe", "act", "dve", "pool")


class Res:
    __slots__ = ("name", "writers", "readers", "dma_cnt", "sem")

    def __init__(self, name):
        self.name = name
        self.writers = []
        self.readers = []
        self.dma_cnt = 0
        self.sem = None


class Op:
    __slots__ = ("eng", "fn", "is_dma", "deps", "signal", "sig_val", "dma_res", "dma_val", "dma_inc")

    def __init__(self, eng, fn, is_dma):
        self.eng = eng
        self.fn = fn
        self.is_dma = is_dma
        self.deps = []
        self.signal = False
        self.sig_val = 0
        self.dma_res = None
        self.dma_val = 0
        self.dma_inc = 16


class Sched:
    def __init__(self):
        self.ops = {e: [] for e in ("pe", "act", "dve", "pool", "sp")}
        self.res = {}
        self.overlap = {}

    def add_overlap(self, a_names, b_names):
        for a in a_names:
            for b in b_names:
                self.overlap.setdefault(a, set()).add(b)
                self.overlap.setdefault(b, set()).add(a)

    def R(self, name):
        r = self.res.get(name)
        if r is None:
            r = self.res[name] = Res(name)
        return r

    def op(self, eng, fn, reads=(), writes=(), accs=(), dma=False, inc=16):
        o = Op(eng, fn, dma)
        o.dma_inc = inc
        deps = []
        ov = self.overlap
        rl = [self.R(r) for r in reads]
        wl = [self.R(r) for r in writes]
        al = [self.R(r) for r in accs]
        ra = [self.R(y) for x in reads for y in ov.get(x, ())]
        wa = [self.R(y) for x in list(writes) + list(accs) for y in ov.get(x, ())]
        for r in rl:
            deps.extend(r.writers)
        for r in ra:
            deps.extend(r.writers)
        for r in wl:
            deps.extend(r.writers)
            deps.extend(r.readers)
        for r in al:
            deps.extend(r.readers)
        for r in wa:
            deps.extend(r.writers)
            deps.extend(r.readers)
        if dma:
            assert len(wl) + len(al) == 1
            o.dma_res = (wl + al)[0]
        seen = set()
        for d in deps:
            if id(d) in seen or d is o:
                continue
            seen.add(id(d))
            if (not d.is_dma) and (not dma) and d.eng == eng:
                if eng == "pe" or not SAME_ENGINE_SYNC:
                    continue
            o.deps.append(d)
        for r in rl:
            r.readers.append(o)
        for r in wl:
            r.writers = [o]
            r.readers = []
        for r in al:
            if r.readers:
                r.writers = [o]
                r.readers = []
            else:
                r.writers.append(o)
        if dma:
            o.dma_res.dma_cnt += inc
            o.dma_val = o.dma_res.dma_cnt
        self.ops[eng].append(o)
        return o

    def fence(self, eng, names):
        self.op(eng, lambda e: e.nop(), writes=list(names))

    def finalize(self, nc, es):
        for lst in self.ops.values():
            for o in lst:
                for d in o.deps:
                    if not d.is_dma:
                        d.signal = True
        for e in COMPUTE:
            c = 0
            for o in self.ops[e]:
                if (not o.is_dma) and o.signal:
                    c += 1
                    o.sig_val = c
        sems = {e: es.enter_context(nc.semaphore("s_" + e)) for e in COMPUTE}
        nd = 0
        for r in self.res.values():
            if r.dma_cnt:
                r.sem = es.enter_context(nc.semaphore("d%d" % nd))
                nd += 1
        block = es.enter_context(nc.Block())
        starters = {"pe": block.tensor, "act": block.scalar, "dve": block.vector,
                    "pool": block.gpsimd, "sp": block.sync}
        for e, starter in starters.items():
            lst = self.ops[e]
            if not lst:
                continue

            def body(eng, lst=lst):
                seen = {}
                for o in lst:
                    need = {}
                    for d in o.deps:
                        if d.is_dma:
                            key, sem, val = ("d", id(d.dma_res)), d.dma_res.sem, d.dma_val
                        else:
                            key, sem, val = ("e", d.eng), sems[d.eng], d.sig_val
                        if seen.get(key, 0) >= val:
                            continue
                        if key not in need or need[key][1] < val:
                            need[key] = (sem, val)
                    for key, (sem, val) in need.items():
                        eng.wait_ge(sem, val)
                        seen[key] = val
                    ins = o.fn(eng)
                    if o.is_dma:
                        ins.then_inc(o.dma_res.sem, o.dma_inc)
                    elif o.signal:
                        ins.then_inc(sems[o.eng], 1)

            starter(body)
        return nd


def build(layers, dbgspec=None):
    nc = bass.Bass("TRN2", target_bir_lowering=False)
    S = Sched()
    es = ExitStack()
    L0 = 0 in layers
    L1 = 1 in layers

    def din(name, shape, dt=F32):
        return nc.dram_tensor(name, list(shape), dt, kind="ExternalInput").ap()

    def dout(name, shape, dt=F32):
        return nc.dram_tensor(name, list(shape), dt, kind="ExternalOutput").ap()

    def dscr(name, shape, dt=BF16):
        return nc.dram_tensor(name, list(shape), dt, kind="Internal").ap()

    def sb(name, shape, dt):
        return es.enter_context(nc.sbuf_tensor(name, list(shape), dt))

    def dma(out, in_, reads, write, slow=False, acc=False):
        kw = {"allow_slow_non_contiguous": True} if slow else {}
        if acc:
            return S.op("sp", lambda e: e.dma_start(out=out, in_=in_, **kw), reads=reads, accs=[write], dma=True)
        return S.op("sp", lambda e: e.dma_start(out=out, in_=in_, **kw), reads=reads, writes=[write], dma=True)

    xo = din("xo", [4096, D])
    xoth = din("xoth", [4096, D])
    ctx_in = din("ctx", [256, D])
    cvec = din("cvec", [2, D])
    cs_own = din("cs_own", [4096, 64])
    cs_oth = din("cs_oth", [4096, 64])
    ident_d = din("ident", [128, 128])
    if L0:
        xhalo = din("xhalo", [256, D])
        cs_halo = din("cs_halo", [256, 64])
        wmask_d = din("wmask", [4, 128, 128])
    W = {}
    for l in layers:
        W[l] = dict(
            mod_w=din("mod_w%d" % l, [D, 6 * D]), mod_b=din("mod_b%d" % l, [1, 6 * D]),
            nmw=din("nmw%d" % l, [1, D]), nfw=din("nfw%d" % l, [1, D]),
            w_in=din("w_in%d" % l, [D, 1536 if l == 0 else 3072]),
            w_out=din("w_out%d" % l, [D, D]),
            fw_in=din("fw_in%d" % l, [D, 2 * FH]), fw_out=din("fw_out%d" % l, [FH, D]),
            gv=din("gv%d" % l, [4 if l == 0 else 2, 64]),
            wq_s=dscr("wq_s%d" % l, [128, 8, 1024]),
            wkv_s=dscr("wkv_s%d" % l, [128, 8, 512 if l == 0 else 2048]),
            wout_s=[dscr("wout_s%d_%d" % (l, v), [128, 8, 1024]) for v in range(2)],
            fwin_s=dscr("fwin_s%d" % l, [NJ, 128, 2, 8, 128]),
            fwout_s=[dscr("fwout_s%d_%d" % (l, v), [NJ, 128, D]) for v in range(2)],
        )
        if l == 0:
            W[l]["sink"] = din("sink0", [1, 8])
        else:
            W[l]["lv"] = din("lv1", [4, 64])
            W[l]["subln"] = din("subln1", [1, 128])
    y_out = dout("y", [4096, D])
    fused = L0 and L1
    if fused:
        x1own_t = nc.dram_tensor("x1own", [4096, D], F32)
        x1all_t = nc.dram_tensor("x1all", [8, 1024, D], F32)
        x1own, x1all = x1own_t.ap(), x1all_t.ap()
        ctx1s = dscr("ctx1s", [256, D], F32)
        cs_full = din("cs_full", [8192, 64])
    ctx_out = dout("ctx1", [256, D]) if (L0 and not L1) else None
    if L1:
        kt1 = dscr("kt1", [8, 128, 66 * 128])
        v1 = dscr("v1", [8, 128, 66, 128])
    dbg = {}

    identf = sb("identf", [128, 128], F32)
    identb = sb("identb", [128, 128], BF16)
    onesf = sb("onesf", [128, 128], F32)
    onesb = sb("onesb", [128, 128], BF16)
    eps_t = sb("eps_t", [128, 1], F32)
    junk = sb("junk", [128, 1024], BF16)
    small = sb("small", [128, 128], F32)
    nT = sb("nT", [128, 2, 8], F32)
    MOD = {}
    for l in layers:
        MOD[l] = dict(modT=sb("modT%d" % l, [128, 4, 8, 2], F32), G1=sb("G1_%d" % l, [128, 8, 2], F32),
                      G2=sb("G2_%d" % l, [128, 8, 2], F32), gq=sb("gq%d" % l, [128, 2, 64], F32),
                      gk=sb("gk%d" % l, [128, 2, 64], F32))
    xc = sb("xc", [128, 4, D], F32)
    xs = sb("xs", [128, D], BF16)
    hT = sb("hT", [128, 8, 512], BF16)
    cst = sb("cst", [128, 4, 64], F32)
    ybuf = sb("ybuf", [128, 1024], F32)
    obuf = sb("obuf", [128, 1024], F32)
    tt = sb("tt", [128, 1024], F32)
    t1 = tt[:, 0:512]
    t2 = tt[:, 512:1024]
    qkb = sb("qkb", [128, 1024], BF16)
    wmix = sb("wmix", [128, 8, 2048], BF16)
    ovA = sb("ovA", [128, 11264], BF16)
    QT = ovA[:, 0:4096].rearrange("p (c n) -> p c n", c=8)
    OT = ovA[:, 4096:8192].rearrange("p (c n) -> p c n", c=8)
    PT = ovA[:, 8192:11264].rearrange("p (b m n) -> p b m n", b=3, m=2)
    actT = ovA[:, 0:NJ * 512].rearrange("p (j n) -> p j n", j=NJ)
    OVA = ["QT", "OT", "PT0", "PT1", "PT2", "actT"]
    epi = sb("epi", [128, 4, 512], F32)
    pss = tt[:].bitcast(BF16).rearrange("p (h m n) -> p h m n", h=2, m=2)
    GS = 16
    fwi = sb("fwi", [128, 2, 2, 8, 128], BF16)
    fwo = sb("fwo", [128, 2, 2, D], BF16)
    sg = sb("sg", [128, 2, 256], F32)
    sgf = sg[:].rearrange("p a n -> p (a n)")
    if L0:
        wm = sb("wm", [128, 4, 128], BF16)
        esk = sb("esk", [128, 8], F32)
    if L1:
        neglam = sb("neglam", [128, 1], F32)
        subw = sb("subw", [128, 1], F32)
    KVN = 39936
    kvr = sb("kvr", [128, KVN], BF16)
    stage = kvr[:, 0:16384].bitcast(F32)
    stb = kvr[:, 16384:24576]
    mrow = kvr[0:2, 24576:36864].bitcast(F32)
    Gbc = kvr[:, 36864:38912].bitcast(F32)
    sel = kvr[0:2, 38912:39424].bitcast(F32).rearrange("p (r m) -> p r m", r=2)
    PRO = ["stage", "stb", "stageA", "stageB", "stbA", "stbB", "mrowv", "mrow0", "mrow1", "gbc"] + ["mws%d" % i for i in range(8)]
    if L0:
        KA = kvr[:, 0:4608]
        KB = kvr[:, 4608:13056]
        VAc = kvr[:, 13056:19968].rearrange("p (n d) -> p n d", d=192)
        VBc = kvr[:, 19968:32640].rearrange("p (n d) -> p n d", d=192)
        VA0, VA1 = VAc[:, :, 0:128], VAc[:, :, 64:192]
        VB0, VB1 = VBc[:, :, 0:128], VBc[:, :, 64:192]
    if L1:
        KH = kvr[:, 0:16896].rearrange("p (b n) -> p b n", b=2)
        VH = kvr[:, 16896:33792].rearrange("p (b n d) -> p b n d", b=2, d=128)
        kst = kvr[:, 33792:34816].rearrange("p (c n) -> p c n", c=8)
        vst = kvr[:, 34816:35840]
    MAINKV = ["KA", "KB", "VA0", "VA1", "VB0", "VB1", "KH0", "KH1", "VH0", "VH1", "kst", "vst"]
    ps = es.enter_context(nc.psum_tensor("ps", [128, 4096], F32))

    def bank(i, n=512, p0=0, p1=128):
        return ps[p0:p1, i * 512:i * 512 + n]

    def bankb(i):
        return ps[:, i * 512:(i + 1) * 512].bitcast(BF16)

    def PB(i):
        return "ps%d" % i

    B_chk = [lambda name: None]
    S.add_overlap(["stage"], ["mws%d" % i for i in range(8)] + ["stageA", "stageB"])
    S.add_overlap(["stb"], ["stbA", "stbB"])
    S.add_overlap(["stageA"], ["mws%d" % i for i in range(8)])
    S.add_overlap(["pss0", "pss1"], ["t1", "t2"])
    for _bk in range(4, 8):
        S.add_overlap(["bc%d" % _bk], ["ps%d" % _bk])
    S.add_overlap(PRO, MAINKV)
    S.add_overlap(["actT"], ["QT", "OT", "PT0", "PT1", "PT2"])
    dma(identf[:], ident_d, [], "identf")
    S.op("dve", lambda e: e.tensor_copy(out=identb[:], in_=identf[:]), ["identf"], ["identb"])
    S.op("dve", lambda e: e.memset(onesf[:], 1.0), [], ["onesf"])
    S.op("dve", lambda e: e.memset(onesb[:], 1.0), [], ["onesb"])
    S.op("dve", lambda e: e.memset(eps_t[:], EPS), [], ["eps"])
    S.op("dve", lambda e: e.tensor_copy(out=sel[0:2, :, :],
                                        in_=identf[0:2, 0:2].unsqueeze(2).broadcast_to([2, 2, 128])),
         ["identf"], ["sel"])
    if L0:
        wmv = stage[:, 0:512].rearrange("p (m q) -> p m q", m=4)
        dma(wmv, wmask_d.rearrange("m k q -> k m q"), [], "stage")
        S.op("dve", lambda e: e.tensor_copy(out=wm[:], in_=wmv), ["stage"], ["wm"])

    def emit_mod(l):
        w, M = W[l], MOD[l]
        modT, G1, G2 = M["modT"], M["G1"], M["G2"]
        scT = small[:, 0:16].rearrange("p (k r) -> p k r", r=2)
        for r in range(2):
            dma(scT[:, :, r], cvec[r].rearrange("(k p) -> p k", p=128), ["scT"], "scin%d" % r, slow=True)
        S.op("act", lambda e: e.activation(out=scT, in_=scT, func=AF.Silu), ["scin0", "scin1"], ["scT"])
        for r in range(2):
            dma(mrow[r:r + 1, :], w["mod_b"], ["mrowv"], "mrow%d" % r)
        mws = [stage[:, i * 512:(i + 1) * 512] for i in range(8)]
        it = 0
        for n in range(12):
            for k in range(8):
                buf = it % 8
                it += 1
                dma(mws[buf], w["mod_w"][k * 128:(k + 1) * 128, n * 512:(n + 1) * 512], [], "mws%d" % buf)
                S.op("pe", lambda e, k=k, buf=buf: e.matmul(bank(0, 512, 0, 2), lhsT=scT[:, k, :], rhs=mws[buf],
                                                            start=(k == 0), stop=(k == 7)),
                     ["scT", "mws%d" % buf], [], accs=[PB(0)])
            S.op("dve", lambda e, n=n: e.tensor_tensor(out=mrow[:, n * 512:(n + 1) * 512], in0=bank(0, 512, 0, 2),
                                                       in1=mrow[:, n * 512:(n + 1) * 512], op=ALU.add),
                 [PB(0), "mrow0", "mrow1"], [], accs=["mrowv"])
        pst = bank(1, 64).rearrange("p (v k r) -> p v k r", v=4, k=8)
        for vi, v in enumerate((0, 1, 3, 4)):
            for k in range(8):
                S.op("pe", lambda e, vi=vi, v=v, k=k: e.transpose(
                    out=pst[:, vi, k, :], in_=mrow[:, v * 1024 + k * 128:v * 1024 + (k + 1) * 128],
                    identity=identf[0:2, 0:2]), ["mrowv", "identf"], [], accs=[PB(1)])
        S.op("dve", lambda e: e.tensor_copy(out=modT[:], in_=pst), [PB(1)], ["modT%d" % l])
        dma(nT[:, 0, :], w["nmw"][0].rearrange("(k p) -> p k", p=128), ["G%d" % l], "nT0", slow=True)
        dma(nT[:, 1, :], w["nfw"][0].rearrange("(k p) -> p k", p=128), ["G%d" % l], "nT1", slow=True)
        for gi, (G, vsc) in enumerate(((G1, 1), (G2, 3))):
            S.op("dve", lambda e, G=G, vsc=vsc: e.tensor_scalar(out=G[:], in0=modT[:, vsc], scalar1=1.0,
                                                                scalar2=None, op0=ALU.add),
                 ["modT%d" % l], ["G%d" % l])
            S.op("dve", lambda e, G=G, gi=gi: e.tensor_tensor(
                out=G[:], in0=G[:], in1=nT[:, gi, :].unsqueeze(2).broadcast_to([128, 8, 2]), op=ALU.mult),
                ["G%d" % l, "nT0", "nT1"], ["G%d" % l])
        gq, gk = M["gq"], M["gk"]
        if l == 0:
            dma(gq[:, 0, :], w["gv"][0].partition_broadcast(128), [], "gains%d" % l, acc=True)
            dma(gk[:, 0, :], w["gv"][1].partition_broadcast(128), [], "gains%d" % l, acc=True)
            dma(gq[:, 1, :], w["gv"][2].partition_broadcast(128), [], "gains%d" % l, acc=True)
            dma(gk[:, 1, :], w["gv"][3].partition_broadcast(128), [], "gains%d" % l, acc=True)
            dma(esk[:], w["sink"][0].partition_broadcast(128), [], "eskin")
            S.op("act", lambda e: e.activation(out=esk[:], in_=esk[:], func=AF.Exp), ["eskin"], ["esk"])
        else:
            dma(gq[:, 0, :], w["gv"][0].partition_broadcast(128), [], "gains%d" % l, acc=True)
            dma(gk[:, 0, :], w["gv"][1].partition_broadcast(128), [], "gains%d" % l, acc=True)
            lvt = small[:, 64:128].rearrange("p (a b) -> p a b", a=1)
            lam_init = 0.8 - 0.6 * math.exp(-0.3 * l)
            lq = ybuf[:, 0:256].rearrange("p (a d) -> p a d", a=4)
            dma(lq, w["lv"].partition_broadcast(128), ["ybuf"], "ybuf")
            dd = small[:, 32:36]
            S.op("dve", lambda e: e.tensor_tensor(out=obuf[:, 0:128].rearrange("p (a d) -> p a d", a=2),
                                                  in0=ybuf[:, 0:256].rearrange("p (a t d) -> p a t d", a=2, t=2)[:, :, 0, :],
                                                  in1=ybuf[:, 0:256].rearrange("p (a t d) -> p a t d", a=2, t=2)[:, :, 1, :],
                                                  op=ALU.mult), ["ybuf"], ["obuf"])
            S.op("dve", lambda e: e.tensor_reduce(out=dd[:, 0:2], in_=obuf[:, 0:128].rearrange("p (a d) -> p a d", a=2),
                                                  axis=AX.X, op=ALU.add), ["obuf"], ["dd"])
            S.op("act", lambda e: e.activation(out=dd[:, 0:2], in_=dd[:, 0:2], func=AF.Exp), ["dd"], ["dd"])
            S.op("dve", lambda e: e.scalar_tensor_tensor(out=neglam[:], in0=dd[:, 1:2], scalar=-lam_init,
                                                         in1=dd[:, 0:1], op0=ALU.add, op1=ALU.subtract),
                 ["dd"], ["neglam"])
            dma(subw[:], w["subln"][0].rearrange("(p o) -> p o", o=1), [], "subwin", slow=True)
            S.op("dve", lambda e: e.tensor_scalar(out=subw[:], in0=subw[:], scalar1=(1.0 - lam_init), scalar2=None,
                                                  op0=ALU.mult), ["subwin"], ["subw"])

    def gate_bc(gidx, r):
        for n in range(2):
            S.op("pe", lambda e, n=n: e.matmul(bank(2 + n), lhsT=sel[0:2, r, :],
                                               rhs=mrow[:, gidx * 1024 + n * 512:gidx * 1024 + (n + 1) * 512],
                                               start=True, stop=True), ["mrowv", "sel"], [PB(2 + n)])
            S.op("act", lambda e, n=n: e.copy(out=Gbc[:, n * 512:(n + 1) * 512], in_=bank(2 + n)),
                 [PB(2 + n)], [], accs=["gbc"])

    def emit_weights(l):
        w = W[l]
        SA = [stage[:, 0:4096], stage[:, 4096:8192]]
        SB = [stb[:, 0:4096], stb[:, 4096:8192]]
        SN = ["stageA", "stageB"]
        BN = ["stbA", "stbB"]
        cnt = [0]

        def unit(loads, cast, stores):
            u = cnt[0] % 2
            cnt[0] += 1
            for dstf, src in loads:
                dma(dstf(SA[u]), src, [], SN[u], acc=True)
            cast(SA[u], SB[u], SN[u], BN[u])
            for dst, srcf, name in stores:
                dma(dst, srcf(SB[u]), [BN[u]], name, acc=True)

        def plain_cast(nel):
            def f(sa, sb_, sn, bn):
                S.op("act", lambda e: e.copy(out=sb_[:, 0:nel], in_=sa[:, 0:nel]), [sn], [bn])
            return f

        def v3(ap, k, n):
            return ap[:, 0:k * n].rearrange("p (k n) -> p k n", k=k)

        wr = w["w_in"]
        for kh in range(2):
            rows = slice(kh * 512, (kh + 1) * 512)
            if l == 0:
                def qcast(sa, sb_, sn, bn):
                    s3, d3 = v3(sa, 4, 1024), v3(sb_, 4, 1024)
                    for grp in range(2):
                        for half in range(2):
                            dv = d3[:, :, grp * 512:(grp + 1) * 512].rearrange("p k (j h d) -> p k j h d", j=4, h=2)[:, :, :, half, :]
                            sv = s3[:, :, grp * 512:(grp + 1) * 512].rearrange("p k (h j d) -> p k h j d", h=2, j=4)[:, :, half, :, :]
                            S.op("act", lambda e, dv=dv, sv=sv: e.copy(out=dv, in_=sv), [sn], [], accs=[bn])
                unit([(lambda sa: v3(sa, 4, 1024)[:, :, 0:512], wr[rows, 0:512].rearrange("(k p) n -> p k n", p=128)),
                      (lambda sa: v3(sa, 4, 1024)[:, :, 512:1024], wr[rows, 768:1280].rearrange("(k p) n -> p k n", p=128))],
                     qcast, [(w["wq_s"][:, kh * 4:(kh + 1) * 4, :], lambda sb_: v3(sb_, 4, 1024), "wq_s%d" % l)])
            else:
                unit([(lambda sa: v3(sa, 4, 1024), wr[rows, 0:1024].rearrange("(k p) n -> p k n", p=128))],
                     plain_cast(4096), [(w["wq_s"][:, kh * 4:(kh + 1) * 4, :], lambda sb_: v3(sb_, 4, 1024), "wq_s%d" % l)])
        if l == 0:
            unit([(lambda sa, i=i: v3(sa, 8, 512)[:, :, i * 128:(i + 1) * 128], wr[:, c0:c0 + 128].rearrange("(k p) n -> p k n", p=128))
                  for i, c0 in enumerate((512, 1280, 640, 1408))],
                 plain_cast(4096), [(w["wkv_s"], lambda sb_: v3(sb_, 8, 512), "wkv_s%d" % l)])
        else:
            for part in range(2):
                for kh in range(2):
                    rows = slice(kh * 512, (kh + 1) * 512)
                    c0 = 1024 + part * 1024
                    unit([(lambda sa: v3(sa, 4, 1024), wr[rows, c0:c0 + 1024].rearrange("(k p) n -> p k n", p=128))],
                         plain_cast(4096),
                         [(w["wkv_s"][:, kh * 4:(kh + 1) * 4, part * 1024:(part + 1) * 1024], lambda sb_: v3(sb_, 4, 1024),
                           "wkv_s%d" % l)])

        def gate_cast(nj):
            def f(sa, sb_, sn, bn):
                S.op("dve", lambda e: e.tensor_tensor(out=v3(sb_, nj, 1024), in0=v3(sa, nj, 1024),
                                                      in1=Gbc.unsqueeze(1).broadcast_to([128, nj, 1024]), op=ALU.mult),
                     [sn, "gbc"], [bn])
            return f

        for v in range(2 if l == 0 else 1):
            gate_bc(2, v)
            for ch in range(2):
                if l == 0:
                    loads = []
                    for cc in range(4):
                        c = ch * 4 + cc
                        grp, j = c // 4, c % 4
                        for half in range(2):
                            r0 = grp * 512 + (half * 4 + j) * 64
                            loads.append((lambda sa, cc=cc, half=half: v3(sa, 4, 1024)[half * 64:(half + 1) * 64, cc, :],
                                          w["w_out"][r0:r0 + 64, :]))
                else:
                    loads = [(lambda sa: v3(sa, 4, 1024),
                              w["w_out"][ch * 512:(ch + 1) * 512, :].rearrange("(c p) n -> p c n", p=128))]
                unit(loads, gate_cast(4),
                     [(w["wout_s"][v][:, ch * 4:(ch + 1) * 4, :], lambda sb_: v3(sb_, 4, 1024), "wout_s%d_%d" % (l, v))])
        for k in range(8):
            for gu in range(2):
                unit([(lambda sa: sa[:, 0:FH], w["fw_in"][k * 128:(k + 1) * 128, gu * FH:(gu + 1) * FH])],
                     plain_cast(FH),
                     [(w["fwin_s"][:, :, gu, k, :].rearrange("j p c -> p j c"),
                       lambda sb_: sb_[:, 0:FH].rearrange("p (j c) -> p j c", c=128), "fwin_s%d" % l)])
        for v in range(2 if l == 0 else 1):
            gate_bc(5, v)
            for j0 in range(0, NJ, 4):
                nj = min(4, NJ - j0)
                unit([(lambda sa, nj=nj: v3(sa, nj, 1024),
                       w["fw_out"][j0 * 128:(j0 + nj) * 128, :].rearrange("(j p) n -> p j n", p=128))],
                     gate_cast(nj),
                     [(w["fwout_s"][v][j0:j0 + nj].rearrange("j p n -> p j n"), lambda sb_, nj=nj: v3(sb_, nj, 1024),
                       "fwout_s%d_%d" % (l, v))])

    def front_parts(l, which, s, r, col0):
        M = MOD[l]
        G = M["G1"] if which == 0 else M["G2"]
        shv = 0 if which == 0 else 2
        modT = M["modT"]
        ss = small[:, 40:41]
        sd = small[:, 41:42]
        rstd = small[:, 42:43]
        pT = bankb(0)
        hname = "hT%d" % (col0 // 128)

        def fa():
            S.op("act", lambda e: e.activation(out=junk[:], in_=xc[:, s, :], func=AF.Square, accum_out=ss),
                 ["xc%d" % s], ["junk", "ss"])
            S.op("act", lambda e: e.activation(out=sd, in_=ss, func=AF.Sqrt, bias=eps_t[:, 0:1], scale=1.0 / D),
                 ["ss", "eps"], ["sd"])
            S.op("dve", lambda e: e.reciprocal(out=rstd, in_=sd), ["sd"], ["rstd"])
            S.op("dve", lambda e: e.tensor_scalar(out=xs[:], in0=xc[:, s, :], scalar1=rstd, scalar2=None, op0=ALU.mult),
                 ["xc%d" % s, "rstd"], ["xs"])

        def fT():
            for k in range(8):
                S.op("pe", lambda e, k=k: e.transpose(out=pT[:, k * 128:(k + 1) * 128], in_=xs[:, k * 128:(k + 1) * 128],
                                                      identity=identb[:]), ["xs", "identb"], [], accs=[PB(0)])

        def fb():
            for k in range(8):
                S.op("dve", lambda e, k=k: e.tensor_scalar(out=hT[:, k, col0:col0 + 128], in0=pT[:, k * 128:(k + 1) * 128],
                                                           scalar1=G[:, k, r:r + 1], scalar2=modT[:, shv, k, r:r + 1],
                                                           op0=ALU.mult, op1=ALU.add),
                     [PB(0), "G%d" % l, "modT%d" % l], [], accs=[hname])

        return fa, fT, fb, hname

    def front(l, which, s, r, col0):
        fa, fT, fb, hname = front_parts(l, which, s, r, col0)
        fa()
        fT()
        fb()
        return hname

    def normrope(src, hshape, gain, rope, dst, srcres, dstres, gres, part=0):
        nh = 1
        for a in hshape:
            nh *= a
        n = nh * 64
        ss = small[:, 44:44 + nh]
        sd = small[:, 64:64 + nh]
        rs = small[:, 84:84 + nh]

        def hv(ap):
            if len(hshape) == 1:
                return ap.rearrange("p (h d) -> p h d", d=64)
            return ap.rearrange("p (a b d) -> p a b d", a=hshape[0], d=64)

        if part in (0, 1):
            S.op("act", lambda e: e.activation(out=obuf[:, 0:n], in_=src, func=AF.Square), srcres, ["obuf"])
            S.op("dve", lambda e: e.tensor_reduce(out=ss, in_=obuf[:, 0:n].rearrange("p (h d) -> p h d", d=64),
                                                  axis=AX.X, op=ALU.add), ["obuf"], ["nss"])
        if part == 1:
            return
        S.op("act", lambda e: e.activation(out=sd, in_=ss, func=AF.Sqrt, bias=eps_t[:, 0:1], scale=1.0 / 64),
             ["nss", "eps"], ["nsd"])
        S.op("dve", lambda e: e.reciprocal(out=rs, in_=sd), ["nsd"], ["nrs"])
        S.op("dve", lambda e: e.tensor_tensor(out=hv(ybuf[:, 0:n]), in0=hv(src), in1=gain, op=ALU.mult),
             srcres + [gres], ["ybuf"])
        if rope:
            y4 = ybuf[:, 0:n].rearrange("p (h i t) -> p h i t", i=32, t=2)
            o4 = obuf[:, 0:n].rearrange("p (h i t) -> p h i t", i=32, t=2)
            C = cst[:, rope - 1, 0:32].unsqueeze(1).broadcast_to([128, nh, 32])
            Sn = cst[:, rope - 1, 32:64].unsqueeze(1).broadcast_to([128, nh, 32])
            a1 = t1[:, 0:nh * 32].rearrange("p (h i) -> p h i", i=32)
            a2 = t2[:, 0:nh * 32].rearrange("p (h i) -> p h i", i=32)
            cr = "cst%d" % (rope - 1)
            S.op("dve", lambda e: e.tensor_tensor(out=a1, in0=y4[:, :, :, 0], in1=C, op=ALU.mult), ["ybuf", cr], ["t1"])
            S.op("dve", lambda e: e.tensor_tensor(out=a2, in0=y4[:, :, :, 1], in1=Sn, op=ALU.mult), ["ybuf", cr], ["t2"])
            S.op("dve", lambda e: e.tensor_tensor(out=o4[:, :, :, 0], in0=a1, in1=a2, op=ALU.subtract),
                 ["t1", "t2", "nss"], ["obuf"])
            S.op("dve", lambda e: e.tensor_tensor(out=a1, in0=y4[:, :, :, 0], in1=Sn, op=ALU.mult), ["ybuf", cr], ["t1"])
            S.op("dve", lambda e: e.tensor_tensor(out=a2, in0=y4[:, :, :, 1], in1=C, op=ALU.mult), ["ybuf", cr], ["t2"])
            S.op("dve", lambda e: e.tensor_tensor(out=o4[:, :, :, 1], in0=a1, in1=a2, op=ALU.add),
                 ["t1", "t2"], [], accs=["obuf"])
            fin, finres = obuf, "obuf"
        else:
            fin, finres = ybuf, "ybuf"
        S.op("dve", lambda e: e.tensor_tensor(out=dst.rearrange("p (h d) -> p h d", d=64),
                                              in0=fin[:, 0:n].rearrange("p (h d) -> p h d", d=64),
                                              in1=rs.unsqueeze(2).broadcast_to([128, nh, 64]), op=ALU.mult),
             [finres, "nrs"], [dstres])

    def load_w(dst, src, name, rd):
        dma(dst, src, [rd], name)

    def attn_unit(qT, nq, kch, mode, out, kvres, sink=None, par=None, hook=None, defer=False, blk4=False, hooks=()):
        nk = len(kch)
        A0, A1 = (4, 5) if par is None else (4 + 2 * par, 5 + 2 * par)
        if par is None:
            bc0 = bank(6, nq, 0, 64)
            bc1f = bank(7, nq)
            bc1 = bank(7, nq, 64, 128)
            bcn0, bcn1 = PB(6), PB(7)
            ei = 0
        else:
            assert nq <= 256 and mode == "aug"
            bc0 = ps[0:64, A0 * 512 + 256:A0 * 512 + 256 + nq]
            bc1f = ps[:, A1 * 512 + 256:A1 * 512 + 256 + nq]
            bc1 = ps[64:128, A1 * 512 + 256:A1 * 512 + 256 + nq]
            bcn0, bcn1 = "bc%d" % A0, "bc%d" % A1
            ei = 2 * par
        sfx = "" if par is None else "_%d" % par

        def Sv(s):
            return ps[:, (2 * s) * 512:(2 * s + 2) * 512].rearrange("p (m n) -> p m n", m=2)[:, :, 0:nq]

        def qk(kc):
            s = kc % 2
            kT, _, _, mask = kch[kc]
            for m in range(2):
                lo = m * 64
                S.op("pe", lambda e, m=m, lo=lo, s=s: e.matmul(bank(2 * s + m, nq), lhsT=kT[lo:lo + 64, :],
                                                               rhs=qT[lo:lo + 64], start=True, stop=(mask is None)),
                     ["QT"] + kvres, [PB(2 * s + m)])
                if mask is not None:
                    mrhs = mask.unsqueeze(1).broadcast_to([128, 4, 128]) if blk4 else mask
                    S.op("pe", lambda e, m=m, s=s, mrhs=mrhs: e.matmul(bank(2 * s + m, nq), lhsT=identb[:], rhs=mrhs,
                                                                       start=False, stop=True),
                         ["identb", "wm"], [], accs=[PB(2 * s + m)])

        qk(0)
        if nk > 1:
            qk(1)
        pend = []
        for kc in range(nk):
            if hook is not None and kc == min(3, nk - 1):
                hook()
            for hk, hf in hooks:
                if kc == min(hk, nk - 1):
                    hf()
            s, b = kc % 2, kc % 3
            _, v0, v1, _ = kch[kc]
            S.op("act", lambda e, s=s, b=b: e.activation(out=PT[:, b, :, 0:nq], in_=Sv(s), func=AF.Exp, scale=SCALE),
                 [PB(2 * s), PB(2 * s + 1)], ["PT%d" % b])
            if kc + 2 < nk:
                qk(kc + 2)
            st, sp_ = (kc == 0), (kc == nk - 1)
            if mode == "aug":
                S.op("pe", lambda e, b=b, v0=v0, st=st, sp_=sp_: e.matmul(bank(A0, nq), lhsT=v0, rhs=PT[:, b, 0, 0:nq],
                                                                          start=st, stop=sp_),
                     ["PT%d" % b] + kvres, [PB(A0)] if st else [], accs=[] if st else [PB(A0)])
                S.op("pe", lambda e, b=b, v1=v1, st=st, sp_=sp_: e.matmul(bank(A1, nq), lhsT=v1, rhs=PT[:, b, 1, 0:nq],
                                                                          start=st, stop=sp_),
                     ["PT%d" % b] + kvres, [PB(A1)] if st else [], accs=[] if st else [PB(A1)])
            else:
                for m in range(2):
                    S.op("pe", lambda e, b=b, m=m, v0=v0, st=st, sp_=sp_: e.matmul(bank(4 + m, nq), lhsT=v0,
                                                                                  rhs=PT[:, b, m, 0:nq], start=st, stop=sp_),
                         ["PT%d" % b] + kvres, [PB(4 + m)] if st else [], accs=[] if st else [PB(4 + m)])
                g0 = (kc // GS) * GS
                gsz = min(GS, nk - g0)
                p = kc - g0
                if p % 2 == 1:
                    half = 0 if p == 1 else 1
                    bprev = (kc - 1) % 3
                    S.op("dve", lambda e, b=b, bprev=bprev, half=half: e.tensor_tensor(
                        out=pss[:, half, :, 0:nq], in0=PT[:, bprev, :, 0:nq], in1=PT[:, b, :, 0:nq], op=ALU.add),
                        ["PT%d" % b, "PT%d" % bprev], ["pss%d" % half])
                    if half == 1:
                        S.op("dve", lambda e: e.tensor_tensor(out=pss[:, 0, :, 0:nq], in0=pss[:, 0, :, 0:nq],
                                                              in1=pss[:, 1, :, 0:nq], op=ALU.add), ["pss0", "pss1"], ["pss0"])
                if p == gsz - 1:
                    if gsz == 1:
                        srcf, sres = (lambda m, b=b: PT[:, b, m, 0:nq]), ["PT%d" % b]
                    else:
                        if gsz % 2 == 1:
                            S.op("dve", lambda e, b=b: e.tensor_tensor(out=pss[:, 0, :, 0:nq], in0=pss[:, 0, :, 0:nq],
                                                                       in1=PT[:, b, :, 0:nq], op=ALU.add),
                                 ["pss0", "PT%d" % b], ["pss0"])
                        srcf, sres = (lambda m: pss[:, 0, m, 0:nq]), ["pss0"]
                    gst, gsp = (g0 == 0), (g0 + gsz == nk)
                    pend.append((srcf, sres, gst, gsp, kc))
                while pend and (pend[0][4] < kc or kc == nk - 1):
                    srcf, sres, gst, gsp, _ = pend.pop(0)
                    for m in range(2):
                        S.op("pe", lambda e, m=m, srcf=srcf, gst=gst, gsp=gsp: e.matmul(bank(6 + m, nq), lhsT=onesb[:],
                                                                                        rhs=srcf(m), start=gst, stop=gsp),
                             sres + ["onesb"], [PB(6 + m)] if gst else [], accs=[] if gst else [PB(6 + m)])
        e2_only = [True]

        def epilogue():
            if mode == "aug":
                rr = epi[:, ei, 0:nq]
                bcs = epi[:, ei + 1, 0:nq]
                for m, (row, bk) in enumerate(((64, A0), (0, A1))):
                    if sink is not None and blk4:
                        S.op("dve", lambda e, m=m, row=row, bk=bk: e.tensor_tensor(
                            out=rr[row:row + 1, :].rearrange("p (j q) -> p j q", j=4),
                            in0=bank(bk, nq, row, row + 1).rearrange("p (j q) -> p j q", j=4),
                            in1=esk[row:row + 1, 4 * m:4 * m + 4].unsqueeze(2).broadcast_to([1, 4, 128]), op=ALU.add),
                            [PB(bk), "esk"], ["rr%d" % m + sfx])
                        S.op("act", lambda e, row=row: e.activation(out=rr[row:row + 1, :], in_=rr[row:row + 1, :], func=AF.Ln),
                             ["rr%d" % m + sfx], ["rr%d" % m + sfx])
                    elif sink is not None:
                        S.op("dve", lambda e, m=m, row=row, bk=bk: e.tensor_scalar(
                            out=rr[row:row + 1, :], in0=bank(bk, nq, row, row + 1),
                            scalar1=esk[row:row + 1, sink[m]:sink[m] + 1], scalar2=None, op0=ALU.add),
                            [PB(bk), "esk"], ["rr%d" % m + sfx])
                        S.op("act", lambda e, row=row: e.activation(out=rr[row:row + 1, :], in_=rr[row:row + 1, :], func=AF.Ln),
                             ["rr%d" % m + sfx], ["rr%d" % m + sfx])
                    else:
                        S.op("act", lambda e, row=row, bk=bk: e.activation(out=rr[row:row + 1, :], in_=bank(bk, nq, row, row + 1),
                                                                           func=AF.Ln), [PB(bk)], ["rr%d" % m + sfx])
                    S.op("act", lambda e, row=row: e.activation(out=rr[row:row + 1, :], in_=rr[row:row + 1, :], func=AF.Exp,
                                                                scale=-1.0), ["rr%d" % m + sfx], ["rr%d" % m + sfx])
                S.op("pe", lambda e: e.matmul(bc0, lhsT=onesf[64:65, 0:64], rhs=rr[64:65, :],
                                              start=True, stop=True), ["rr0" + sfx, "onesf"], [bcn0])
                S.op("pe", lambda e: e.matmul(bc1f, lhsT=onesf[0:1, :], rhs=rr[0:1, :], start=True, stop=True),
                     ["rr1" + sfx, "onesf"], [bcn1])
                S.op("act", lambda e: e.copy(out=bcs[0:64, :], in_=bc0), [bcn0], ["bcs0" + sfx])
                S.op("act", lambda e: e.copy(out=bcs[64:128, :], in_=bc1), [bcn1], ["bcs1" + sfx])
                def v4(ap):
                    return ap.rearrange("p (j q) -> p j q", j=4) if blk4 else ap

                S.op("dve", lambda e: e.tensor_tensor(out=out[0:64], in0=v4(bank(A0, nq, 0, 64)), in1=v4(bcs[0:64, :]),
                                                      op=ALU.mult), [PB(A0), "bcs0" + sfx], [], accs=["OT"])
                S.op("dve", lambda e: e.tensor_tensor(out=out[64:128], in0=v4(bank(A1, nq, 64, 128)), in1=v4(bcs[64:128, :]),
                                                      op=ALU.mult), [PB(A1), "bcs1" + sfx], [], accs=["OT"])
            else:
                diff_e1()
                diff_e2()

        r0, r1, o0, o1 = (epi[:, i, 0:nq] for i in range(4))

        def diff_copy():
            S.op("dve", lambda e: e.tensor_copy(out=o0, in_=bank(4, nq)), [PB(4)], ["e_o0"])
            S.op("dve", lambda e: e.tensor_copy(out=o1, in_=bank(5, nq)), [PB(5)], ["e_o1"])

        def diff_e1():
            for m, rbuf in enumerate((r0, r1)):
                S.op("act", lambda e, m=m, rbuf=rbuf: e.activation(out=rbuf, in_=bank(6 + m, nq), func=AF.Ln),
                     [PB(6 + m)], ["e_r%d" % m])
                S.op("act", lambda e, rbuf=rbuf: e.activation(out=rbuf, in_=rbuf, func=AF.Exp, scale=-1.0),
                     ["e_r%d" % m], ["e_r%d" % m])
            S.op("dve", lambda e: e.tensor_tensor(out=o0, in0=o0, in1=r0, op=ALU.mult), ["e_o0", "e_r0"], ["e_o0"])
            S.op("dve", lambda e: e.tensor_tensor(out=o1, in0=o1, in1=r1, op=ALU.mult), ["e_o1", "e_r1"], ["e_o1"])
            S.op("dve", lambda e: e.scalar_tensor_tensor(out=o0, in0=o1, scalar=neglam[:, 0:1], in1=o0,
                                                         op0=ALU.mult, op1=ALU.add), ["e_o0", "e_o1", "neglam"], ["e_o0"])
            S.op("dve", lambda e: e.tensor_tensor(out=r0, in0=o0, in1=o0, op=ALU.mult), ["e_o0"], ["e_r0"])

        def diff_e2():
            S.op("pe", lambda e: e.matmul(bank(6, nq), lhsT=onesf[:], rhs=r0, start=True, stop=True),
                 ["e_r0", "onesf"], [PB(6)])
            S.op("act", lambda e: e.activation(out=r1, in_=bank(6, nq), func=AF.Ln, bias=eps_t[:, 0:1], scale=1.0 / 128),
                 [PB(6), "eps"], ["e_r1"])
            S.op("act", lambda e: e.activation(out=r1, in_=r1, func=AF.Exp, scale=-0.5), ["e_r1"], ["e_r1"])
            S.op("dve", lambda e: e.scalar_tensor_tensor(out=out, in0=o0, scalar=subw[:, 0:1], in1=r1,
                                                         op0=ALU.mult, op1=ALU.mult), ["e_o0", "e_r1", "subw"], [],
                 accs=["OT"])

        if mode == "diff":
            diff_copy()
            if defer:
                return diff_e1, diff_e2
            diff_e1()
            diff_e2()
            return None
        if defer:
            e2_only[0] = None
            epilogue()
            return epilogue
        if par is None:
            epilogue()
            return None
        return epilogue

    SRC_RD = [[]]

    def load_block(src, s, csrc, csl):
        dma(xc[:, s, :], src, SRC_RD[0], "xc%d" % s)
        if csrc is not None:
            dma(cst[:, csl, :], csrc, [], "cst%d" % csl)

    def pass_a(l, blocks):
        w, M = W[l], MOD[l]
        nkv = 512 if l == 0 else 2048
        load_w(wmix[:, :, 0:nkv], w["wkv_s"], "wmix", "wkv_s%d" % l)
        n = len(blocks)
        hnames = {}

        def ld(j):
            if j < n:
                load_block(blocks[j][0], j % 4, blocks[j][1], j % 4)

        fparts = {}

        def fr_a(j):
            if j < n:
                fparts[j] = front_parts(l, 0, j % 4, blocks[j][2], (j % 4) * 128)
                hnames[j] = fparts[j][3]
                fparts[j][0]()

        def fr_T(j):
            if j < n:
                fparts[j][1]()

        def fr_b(j):
            if j < n:
                fparts[j][2]()

        def fr(j):
            fr_a(j)
            fr_T(j)
            fr_b(j)

        def kbanks(j):
            if l == 0:
                return [1] if j % 2 == 0 else [3]
            return [1, 2] if j % 2 == 0 else [5, 6]

        def proj(j, part):
            if j >= n:
                return
            hc = (j % 4) * 128
            hn = hnames[j]
            if l == 0:
                bks, c0 = kbanks(j), 0
            elif part == 0:
                bks, c0 = kbanks(j), 0
            else:
                bks, c0 = [3, 4], 1024
            for bi, bk in enumerate(bks):
                for k in range(8):
                    S.op("pe", lambda e, k=k, bk=bk, bi=bi, hc=hc, c0=c0: e.matmul(
                        bank(bk), lhsT=hT[:, k, hc:hc + 128], rhs=wmix[:, k, c0 + bi * 512:c0 + (bi + 1) * 512],
                        start=(k == 0), stop=(k == 7)),
                        [hn, "wmix"], [PB(bk)] if k == 0 else [], accs=[] if k == 0 else [PB(bk)])

        def post_v(j):
            ib = blocks[j][4]
            S.op("act", lambda e: e.copy(out=vst, in_=ps[:, 1536:2560]), [PB(3), PB(4)], ["vst"])
            dma(v1[:, :, ib, :].rearrange("h p d -> p h d"), vst.rearrange("p (h d) -> p h d", d=128), ["vst"], "v1",
                acc=True)

        def post_k1(j, part):
            if j >= n:
                return
            src, csrc, r, ia, ib = blocks[j]
            rope = (j % 4) + 1 if csrc is not None else 0
            bks = kbanks(j)
            if l == 0:
                gain = M["gk"][:].unsqueeze(2).broadcast_to([128, 2, 2, 64])
                kb_ = bks[0]
                if ia is not None and part == 1:
                    S.op("act", lambda e: e.copy(out=VAc[:, ia, 0:64], in_=bank(kb_)[:, 256:320]), [PB(kb_)], [], accs=["VA0"])
                    S.op("act", lambda e: e.copy(out=VAc[:, ia, 128:192], in_=bank(kb_)[:, 320:384]), [PB(kb_)], [], accs=["VA1"])
                if ib is not None and part == 1:
                    S.op("act", lambda e: e.copy(out=VBc[:, ib, 0:64], in_=bank(kb_)[:, 384:448]), [PB(kb_)], [], accs=["VB0"])
                    S.op("act", lambda e: e.copy(out=VBc[:, ib, 128:192], in_=bank(kb_)[:, 448:512]), [PB(kb_)], [], accs=["VB1"])
                normrope(bank(bks[0], 256), (2, 2), gain, rope, qkb[:, 0:256], [PB(bks[0])], "qkb", "gains%d" % l, part=part)
            else:
                gain = M["gk"][:, 0:1, :].broadcast_to([128, 16, 64])
                srcap = ps[:, bks[0] * 512:bks[0] * 512 + 1024]
                normrope(srcap, (16,), gain, rope, qkb[:, 0:1024], [PB(bks[0]), PB(bks[1])], "qkb", "gains%d" % l, part=part)

        def post_k2(j, part):
            if j < 0 or j >= n:
                return
            src, csrc, r, ia, ib = blocks[j]
            if l == 0:
                pT = bankb(2)
                if part == 0:
                    for t in range(2):
                        S.op("pe", lambda e, t=t: e.transpose(out=pT[:, t * 128:(t + 1) * 128], in_=qkb[:, t * 128:(t + 1) * 128],
                                                              identity=identb[:]), ["qkb", "identb"], [], accs=[PB(2)])
                    return
                if ia is not None:
                    S.op("act", lambda e: e.copy(out=KA[:, ia * 128:(ia + 1) * 128], in_=pT[:, 0:128]), [PB(2)], [], accs=["KA"])
                if ib is not None:
                    S.op("act", lambda e: e.copy(out=KB[:, ib * 128:(ib + 1) * 128], in_=pT[:, 128:256]), [PB(2)], [], accs=["KB"])
            else:
                pT = bankb(7)
                if part == 0:
                    for t in range(8):
                        S.op("pe", lambda e, t=t: e.transpose(out=pT[:, t * 128:(t + 1) * 128], in_=qkb[:, t * 128:(t + 1) * 128],
                                                              identity=identb[:]), ["qkb", "identb"], [], accs=[PB(7)])
                    return
                S.op("act", lambda e: e.copy(out=kst.rearrange("p c n -> p (c n)"), in_=pT), [PB(7)], ["kst"])
                dma(kt1[:, :, ib * 128:(ib + 1) * 128].rearrange("h p n -> p h n"), kst, ["kst"], "kt1", acc=True)

        for j in range(3):
            ld(j)
        fr(0)
        proj(0, 0)
        proj(0, 1) if l == 1 else None
        fr(1)
        for i in range(n):
            ld(i + 3)
            post_k1(i, 1)
            fr_a(i + 2)
            post_k2(i - 1, 0)
            proj(i + 1, 0)
            fr_T(i + 2)
            if l == 1:
                post_v(i)
                proj(i + 1, 1)
            post_k1(i, 2)
            fr_b(i + 2)
            post_k2(i - 1, 1)
        post_k2(n - 1, 0)
        post_k2(n - 1, 1)

    def pass_b_chunk(l, srcs, css, r, own0, outs, is_ctx, first):
        w, M = W[l], MOD[l]
        nblk = len(srcs)
        nq = nblk * 128
        for i in range(nblk):
            load_block(srcs[i], i, css[i] if css else None, i)
        hq = {}
        qparts = {}

        def qfr_a(j):
            if j < nblk:
                qparts[j] = front_parts(l, 0, j, r, j * 128)
                hq[j] = qparts[j][3]
                qparts[j][0]()

        def qfr_T(j):
            if j < nblk:
                qparts[j][1]()

        def qfr_b(j):
            if j < nblk:
                qparts[j][2]()

        def qproj(j):
            if j >= nblk:
                return
            bks = (1, 2) if j % 2 == 0 else (3, 4)
            for nn in range(2):
                for k in range(8):
                    S.op("pe", lambda e, k=k, nn=nn, j=j, bk=bks[nn]: e.matmul(
                        bank(bk), lhsT=hT[:, k, j * 128:(j + 1) * 128], rhs=wmix[:, k, nn * 512:(nn + 1) * 512],
                        start=(k == 0), stop=(k == 7)),
                        [hq[j], "wmix"], [PB(bks[nn])] if k == 0 else [], accs=[] if k == 0 else [PB(bks[nn])])

        def qnr(j, part):
            if j >= nblk:
                return
            bks = (1, 2) if j % 2 == 0 else (3, 4)
            if l == 0:
                gain = M["gq"][:].unsqueeze(2).broadcast_to([128, 2, 8, 64])
            else:
                gain = M["gq"][:, 0:1, :].broadcast_to([128, 16, 64])
            rope = j + 1 if css else 0
            normrope(ps[:, bks[0] * 512:bks[0] * 512 + 1024], (2, 8) if l == 0 else (16,), gain, rope, qkb[:, 0:1024],
                     [PB(bks[0]), PB(bks[1])], "qkb", "gains%d" % l, part=part)

        def qT(j, part):
            if j < 0 or j >= nblk:
                return
            pT = bankb(5)
            if part == 0:
                for t in range(8):
                    S.op("pe", lambda e, t=t: e.transpose(out=pT[:, t * 128:(t + 1) * 128], in_=qkb[:, t * 128:(t + 1) * 128],
                                                          identity=identb[:]), ["qkb", "identb"], [], accs=[PB(5)])
            else:
                S.op("act", lambda e, j=j: e.copy(out=QT[:, :, j * 128:(j + 1) * 128],
                                                  in_=pT.rearrange("p (c n) -> p c n", c=8)), [PB(5)], [], accs=["QT"])

        qfr_a(0)
        qfr_T(0)
        qfr_b(0)
        qproj(0)
        qfr_a(1)
        qfr_T(1)
        qfr_b(1)
        for i in range(nblk):
            qnr(i, 1)
            qfr_a(i + 2)
            qT(i - 1, 0)
            qproj(i + 1)
            qfr_T(i + 2)
            qnr(i, 2)
            qfr_b(i + 2)
            qT(i - 1, 1)
        qT(nblk - 1, 0)
        qT(nblk - 1, 1)
        B_chk[0]('qproj')
        if l == 0:
            pend_ep = [None]
            ucnt = [0]

            def unit_p(*a, **kw):
                ep = attn_unit(*a, par=ucnt[0] % 2, **kw)
                ucnt[0] += 1
                if pend_ep[0] is not None:
                    pend_ep[0]()
                pend_ep[0] = ep

            def flush_p():
                if pend_ep[0] is not None:
                    pend_ep[0]()
                    pend_ep[0] = None

            if is_ctx:
                for j in range(4):
                    kch = [(KA[:, (34 + t) * 128:(35 + t) * 128], VA0[:, 34 + t, :], VA1[:, 34 + t, :], None) for t in range(2)]
                    unit_p(QT[:, j, 0:nq], nq, kch, "aug", OT[:, j, 0:nq], ["KA", "VA0", "VA1"], sink=(j, 4 + j))
                for j in range(4):
                    kch = [(KB[:, t * 128:(t + 1) * 128], VB0[:, t, :], VB1[:, t, :], None) for t in range(2)]
                    unit_p(QT[:, 4 + j, 0:nq], nq, kch, "aug", OT[:, 4 + j, 0:nq], ["KB", "VB0", "VB1"])
                flush_p()
            else:
                for i in range(nblk):
                    ib = own0 + i
                    prev = ib - 1 if ib > 0 else 32
                    nxt = ib + 1 if ib < NB - 1 else 33
                    mp = 0 if ib == 0 else 1
                    mn = 3 if ib == NB - 1 else 2
                    order = [(34, None), (35, None), (prev, mp), (ib, None), (nxt, mn)]
                    kch = [(KA[:, t * 128:(t + 1) * 128], VA0[:, t, :], VA1[:, t, :],
                            None if mk is None else wm[:, mk, :]) for t, mk in order]
                    attn_unit(QT[:, 0:4, i * 128:(i + 1) * 128], 512, kch, "aug", OT[:, 0:4, i * 128:(i + 1) * 128],
                              ["KA", "VA0", "VA1"], sink=(0, 4), blk4=True)
                kch = [(KB[:, t * 128:(t + 1) * 128], VB0[:, t, :], VB1[:, t, :], None) for t in range(66)]
                for j in range(4):
                    attn_unit(QT[:, 4 + j, 0:nq], nq, kch, "aug", OT[:, 4 + j, 0:nq], ["KB", "VB0", "VB1"])
        else:
            def loadh(h):
                b = h % 2
                dma(KH[:, b, :], kt1[h], ["kt1"], "KH%d" % b)
                dma(VH[:, b, :, :], v1[h], ["v1"], "VH%d" % b)
            loadh(0)
            pend_e = None
            for h in range(8):
                if h + 1 < 8:
                    loadh(h + 1)
                b = h % 2
                kch = [(KH[:, b, t * 128:(t + 1) * 128], VH[:, b, t, :], None, None) for t in range(66)]
                hk = [] if pend_e is None else [(2, pend_e[0]), (6, pend_e[1])]
                pend_e = attn_unit(QT[:, h, 0:nq], nq, kch, "diff", OT[:, h, 0:nq], ["KH%d" % b, "VH%d" % b],
                                   defer=True, hooks=hk)
            pend_e[0]()
            pend_e[1]()
        B_chk[0]('attn')
        if first:
            load_w(wmix[:, :, 1024:2048], w["wout_s"][r], "wmix2", "wout_s%d_%d" % (l, r))
        for i in range(nblk):
            for nn in range(2):
                for c in range(8):
                    S.op("pe", lambda e, c=c, nn=nn, i=i, ob=1 + 2 * (i % 2) + nn: e.matmul(
                        bank(ob), lhsT=OT[:, c, i * 128:(i + 1) * 128], rhs=wmix[:, c, 1024 + nn * 512:1024 + (nn + 1) * 512],
                        start=(c == 0), stop=(c == 7)),
                        ["OT", "wmix2"], [PB(1 + 2 * (i % 2) + nn)] if c == 0 else [],
                        accs=[] if c == 0 else [PB(1 + 2 * (i % 2) + nn)])
                S.op("dve", lambda e, nn=nn, i=i, ob=1 + 2 * (i % 2) + nn: e.tensor_tensor(
                    out=xc[:, i, nn * 512:(nn + 1) * 512], in0=bank(ob), in1=xc[:, i, nn * 512:(nn + 1) * 512], op=ALU.add),
                    [PB(1 + 2 * (i % 2) + nn)], ["xc%d" % i])
        B_chk[0]('oproj')
        fwin_s, fwout_s = w["fwin_s"], w["fwout_s"][r]
        for sub in range(1):
            blks = list(range(nblk))
            ns = len(blks) * 128
            fps = [front_parts(l, 1, i, r, bi * 128) for bi, i in enumerate(blks)]
            for bi in range(len(fps) + 1):
                if bi < len(fps):
                    fps[bi][0]()
                if bi > 0:
                    fps[bi - 1][2]()
                if bi < len(fps):
                    fps[bi][1]()
            hres = ["hT%d" % bi for bi in range(len(blks))]
            dma(fwi[:, 0], fwin_s[0], ["fwin_s%d" % l], "fwi0")
            for j in range(NJ):
                if j + 1 < NJ:
                    dma(fwi[:, (j + 1) % 2], fwin_s[j + 1], ["fwin_s%d" % l], "fwi%d" % ((j + 1) % 2))
                b = j % 2
                for gu in range(2):
                    for k in range(8):
                        S.op("pe", lambda e, k=k, gu=gu, b=b, fb_=1 + 2 * (j % 2): e.matmul(bank(fb_ + gu, ns), lhsT=fwi[:, b, gu, k, :],
                                                                       rhs=hT[:, k, 0:ns], start=(k == 0), stop=(k == 7)),
                             hres + ["fwi%d" % b], [PB(1 + 2 * (j % 2) + gu)] if k == 0 else [],
                             accs=[] if k == 0 else [PB(1 + 2 * (j % 2) + gu)])
                fb_ = 1 + 2 * (j % 2)
                S.op("act", lambda e, fb_=fb_: e.activation(out=sgf[:, 0:ns], in_=bank(fb_, ns), func=AF.Silu), [PB(fb_)], ["sg"])
                S.op("dve", lambda e, j=j, fb_=fb_: e.tensor_tensor(out=actT[:, j, 0:ns], in0=bank(fb_ + 1, ns), in1=sgf[:, 0:ns],
                                                                    op=ALU.mult), [PB(fb_ + 1), "sg"], [], accs=["actT"])
            dma(fwo[:, 0], fwout_s[0:2].rearrange("j p n -> p j n"), ["fwout_s%d_%d" % (l, r)], "fwo0")
            for jp in range(NJ // 2):
                if jp + 1 < NJ // 2:
                    dma(fwo[:, (jp + 1) % 2], fwout_s[2 * (jp + 1):2 * (jp + 1) + 2].rearrange("j p n -> p j n"), ["fwout_s%d_%d" % (l, r)],
                        "fwo%d" % ((jp + 1) % 2))
                b = jp % 2
                for jj in range(2):
                    j = jp * 2 + jj
                    for bi in range(len(blks)):
                        for nn in range(2):
                            bk = bi * 2 + nn
                            S.op("pe", lambda e, j=j, jj=jj, bi=bi, nn=nn, bk=bk, b=b: e.matmul(
                                bank(bk), lhsT=actT[:, j, bi * 128:(bi + 1) * 128], rhs=fwo[:, b, jj, nn * 512:(nn + 1) * 512],
                                start=(j == 0), stop=(j == NJ - 1)),
                                ["actT", "fwo%d" % b], [PB(bk)] if j == 0 else [], accs=[] if j == 0 else [PB(bk)])
            for bi, i in enumerate(blks):
                for nn in range(2):
                    bk = bi * 2 + nn
                    S.op("dve", lambda e, nn=nn, i=i, bk=bk: e.tensor_tensor(out=xc[:, i, nn * 512:(nn + 1) * 512],
                                                                             in0=bank(bk), in1=xc[:, i, nn * 512:(nn + 1) * 512],
                                                                             op=ALU.add), [PB(bk)], ["xc%d" % i])
                dma(outs[i], xc[:, i, :], ["xc%d" % i], "yout%d" % i)

    def blk(t, i):
        return t[i * 128:(i + 1) * 128, :]

    stop = dbgspec.get("stop") if dbgspec else None

    class StopBuild(Exception):
        pass

    def chk(name):
        if stop == name:
            raise StopBuild()

    B_chk[0] = chk
    try:
      for l in layers:
          if stop == "consts":
              break
          emit_mod(l)
          if stop == "mod":
              dma(y_out[0:2, :], mrow[:, 0:1024], ["mrowv"], "yout0")
              dma(y_out[128:256, 0:64], MOD[l]["modT"][:].rearrange("p a b c -> p (a b c)"), ["modT%d" % l], "yout1")
              dma(y_out[256:384, 0:16], MOD[l]["G1"][:].rearrange("p a b -> p (a b)"), ["G%d" % l], "yout2")
              break
          emit_weights(l)
          if stop == "weights":
              break
          if l == 0:
              S.op("pool", lambda e: e.memset(VAc[:, :, 64:128], 0.0), [], ["VA0", "VA1"])
              S.op("pool", lambda e: e.memset(VAc[:, :, 64:65], 1.0), [], ["VA0", "VA1"])
              S.op("pool", lambda e: e.memset(VBc[:, :, 64:128], 0.0), [], ["VB0", "VB1"])
              S.op("pool", lambda e: e.memset(VBc[:, :, 64:65], 1.0), [], ["VB0", "VB1"])
              blocks = [(blk(ctx_in, t), None, 1, 34 + t, t) for t in range(2)]
              blocks += [(blk(xo, t), blk(cs_own, t), 0, t, 2 + t) for t in range(NB)]
              blocks += [(blk(xhalo, t), blk(cs_halo, t), 0, 32 + t, None) for t in range(2)]
              blocks += [(blk(xoth, t), blk(cs_oth, t), 0, None, 34 + t) for t in range(NB)]
              x_src, c_src = xo, ctx_in
              x_dst = x1own if fused else y_out
              c_dst = ctx1s if fused else ctx_out
              SRC_RD[0] = []
          elif fused:
              YO = ["yout0", "yout1", "yout2", "yout3"]
              SRC_RD[0] = YO + ["x1all"]
              blocks = [(blk(ctx1s, t), None, 1, None, t) for t in range(2)]
              for ci in range(8):
                  for rr in range(2):
                      for bl in range(4):
                          pos = rr * 4096 + ci * 512 + bl * 128
                          blocks.append((x1all[ci, rr * 512 + bl * 128:rr * 512 + (bl + 1) * 128, :],
                                         cs_full[pos:pos + 128, :], 0, None, 2 + (ci * 2 + rr) * 4 + bl))
              x_src, c_src, x_dst, c_dst = x1own, ctx1s, y_out, None
          else:
              blocks = [(blk(ctx_in, t), None, 1, None, t) for t in range(2)]
              blocks += [(blk(xo, t), blk(cs_own, t), 0, None, 2 + t) for t in range(NB)]
              blocks += [(blk(xoth, t), blk(cs_oth, t), 0, None, 34 + t) for t in range(NB)]
              x_src, c_src, x_dst, c_dst = xo, ctx_in, y_out, None
              SRC_RD[0] = []
          if dbgspec and dbgspec.get("nblocksA"):
              blocks = blocks[:dbgspec["nblocksA"]]
          pass_a(l, blocks)
          chk("passA")
          load_w(wmix[:, :, 0:1024], W[l]["wq_s"], "wmix", "wq_s%d" % l)
          nchunks = dbgspec.get("nchunks", 8) if dbgspec else 8
          for ci in range(nchunks):
              srcs = [blk(x_src, ci * 4 + i) for i in range(4)]
              css = [blk(cs_own, ci * 4 + i) for i in range(4)]
              outs = [blk(x_dst, ci * 4 + i) for i in range(4)]
              pass_b_chunk(l, srcs, css, 0, ci * 4, outs, False, ci == 0)
              if fused and l == 0:
                  S.op("pool", lambda e, ci=ci: e.collective_compute(
                      "AllGather", ALU.bypass, replica_groups=[[0, 1], [2, 3], [4, 5], [6, 7]],
                      ins=[x1own_t.ap()[ci * 512:(ci + 1) * 512, :].opt()], outs=[x1all_t.ap()[ci].opt()]),
                      ["yout0", "yout1", "yout2", "yout3"], [], accs=["x1all"], dma=True, inc=1)
          if l == 0 and c_dst is not None and not (dbgspec and dbgspec.get("noctx")):
              load_w(wmix[:, :, 1024:2048], W[l]["wout_s"][1], "wmix2", "wout_s%d_1" % l)
              pass_b_chunk(l, [blk(c_src, t) for t in range(2)], None, 1, 0, [blk(c_dst, t) for t in range(2)], True, False)
    except StopBuild:
        pass
    S.op("sp", lambda e: e.nop(), ["yout0", "yout1", "yout2", "yout3"], [])
    nd = S.finalize(nc, es)
    es.close()
    return nc


def _rope_tables(n):
    t = np.arange(n)
    row = (t // 64).astype(np.float32)
    col = (t % 64).astype(np.float32)
    inv = (np.float32(10000.0) ** (-np.arange(16, dtype=np.float32) / np.float32(16))).astype(np.float32)
    ang = np.concatenate([row[:, None] * inv, col[:, None] * inv], axis=-1).astype(np.float32)
    return np.concatenate([np.cos(ang), np.sin(ang)], axis=-1).astype(np.float32)


def _masks(h):
    k = np.arange(128)[:, None]
    q = np.arange(128)[None, :]
    tri_prev = np.where(k >= q, 0.0, NEG).astype(np.float32)
    tri_next = np.where(k <= q, 0.0, NEG).astype(np.float32)
    allneg = np.full((128, 128), NEG, np.float32)
    return np.stack([allneg if h == 0 else tri_prev, tri_prev, tri_next, tri_next if h == 0 else allneg])


_NC_CACHE = {}


def _get_nc(layers):
    key = tuple(layers)
    if key not in _NC_CACHE:
        _NC_CACHE[key] = build(list(layers))
    return _NC_CACHE[key]


def _layer_maps(l, xs_full, ctx_full, inp):
    cs = _rope_tables(8192)
    ident = np.eye(128, dtype=np.float32)
    f = lambda a: np.ascontiguousarray(a, dtype=np.float32)
    maps = []
    for core in range(8):
        b, h = core // 2, core % 2
        o0, t0 = h * 4096, (1 - h) * 4096
        m = {
            "xo": f(xs_full[b, o0:o0 + 4096]), "xoth": f(xs_full[b, t0:t0 + 4096]), "ctx": f(ctx_full[b]),
            "cvec": f(np.stack([inp["c"][b], inp["c_ctx"]])),
            "cs_own": f(cs[o0:o0 + 4096]), "cs_oth": f(cs[t0:t0 + 4096]), "ident": ident,
            "mod_w%d" % l: f(inp["mod_w"][l]), "mod_b%d" % l: f(inp["mod_b"][l][None]),
            "nmw%d" % l: f(inp["norm_mix_w"][l][None]), "nfw%d" % l: f(inp["norm_ffn_w"][l][None]),
            "fw_in%d" % l: f(inp["ffn_w_in"][l]), "fw_out%d" % l: f(inp["ffn_w_out"][l]),
        }
        if l == 0:
            halo = np.zeros((256, D), np.float32)
            csh = np.zeros((256, 64), np.float32)
            if h == 1:
                halo[0:128] = xs_full[b, 4096 - 128:4096]
                csh[0:128] = cs[4096 - 128:4096]
            else:
                halo[128:256] = xs_full[b, 4096:4096 + 128]
                csh[128:256] = cs[4096:4096 + 128]
            m.update({"xhalo": halo, "cs_halo": csh, "wmask": _masks(h),
                      "w_in0": f(inp["ev_w_in"][0]), "w_out0": f(inp["ev_w_out"][0]),
                      "gv0": f(np.stack([inp["ev_qn_a"][0], inp["ev_kn_a"][0], inp["ev_qn_b"][0], inp["ev_kn_b"][0]])),
                      "sink0": f(inp["ev_sink_a"][0][None])})
        else:
            m.update({"w_in1": f(inp["od_w_in"][0]), "w_out1": f(inp["od_w_out"][0]),
                      "gv1": f(np.stack([inp["od_qn"][0], inp["od_kn"][0]])),
                      "lv1": f(np.stack([inp["od_lq1"][0], inp["od_lk1"][0], inp["od_lq2"][0], inp["od_lk2"][0]])),
                      "subln1": f(inp["od_subln"][0][None])})
        maps.append(m)
    return maps


FUSED = True


def kernel(**inputs):
    inp = {k: np.asarray(v) for k, v in inputs.items()}
    x = inp["x"].astype(np.float32, copy=False)
    ctx = inp["ctx"].astype(np.float32, copy=False)
    if FUSED:
        m0 = _layer_maps(0, x, ctx, inp)
        m1 = _layer_maps(1, x, ctx, inp)
        cs = _rope_tables(8192)
        maps = []
        for core in range(8):
            m = dict(m0[core])
            for k, v in m1[core].items():
                if k not in m:
                    m[k] = v
            m["cs_full"] = cs
            maps.append(m)
        res = run_bass_kernel_spmd(_get_nc((0, 1)), maps, core_ids=list(range(8)))
        out = np.stack([np.concatenate([res.results[2 * b]["y"], res.results[2 * b + 1]["y"]], axis=0) for b in range(4)])
        return out.astype(np.float32)
    res0 = run_bass_kernel_spmd(_get_nc((0,)), _layer_maps(0, x, ctx, inp), core_ids=list(range(8)))
    x1 = np.stack([np.concatenate([res0.results[2 * b]["y"], res0.results[2 * b + 1]["y"]], axis=0) for b in range(4)])
    ctx1 = np.stack([res0.results[2 * b]["ctx1"] for b in range(4)])
    res1 = run_bass_kernel_spmd(_get_nc((1,)), _layer_maps(1, x1, ctx1, inp), core_ids=list(range(8)))
    out = np.stack([np.concatenate([res1.results[2 * b]["y"], res1.results[2 * b + 1]["y"]], axis=0) for b in range(4)])
    return out.astype(np.float32)
```
